# Optimizing a Trainium2 kernel written in Bass

```python
import math
import jax, jax.numpy as jnp
from jax import lax
import numpy as np

D_MODEL = 1024
BATCH = 8
SEQ = 4096
DEPTH = 2

HEAD_DIM = 64
D_MIX = D_MODEL
ATTN_WIDTH = D_MIX // 2
ATTN_Q_HEADS = ATTN_WIDTH // HEAD_DIM
ATTN_KV_HEADS = ATTN_Q_HEADS // 4
ATTN_GROUP = ATTN_Q_HEADS // ATTN_KV_HEADS
KV_WIDTH = ATTN_KV_HEADS * HEAD_DIM
WINDOW = 128
BLOCK = 128
ROPE_THETA = 500000.0
ROT_DIM = HEAD_DIM // 4
GM_WIDTH = D_MIX // 4
GM_HEADS = GM_WIDTH // HEAD_DIM
CHUNK = 128
CONV_CH = D_MIX - ATTN_WIDTH - GM_WIDTH
CONV_WIDTH = 31
CONV_PAD = CONV_WIDTH // 2
D_FF = ((8 * D_MODEL // 3 + 255) // 256) * 256
PLE_DIM = 256
EPS = 1e-6
NEG_INF = -1e30

Q_OFF = 0
K_OFF = Q_OFF + ATTN_WIDTH
V_OFF = K_OFF + KV_WIDTH
GM_OFF = V_OFF + KV_WIDTH
CONV_OFF = GM_OFF + 2 * GM_WIDTH
IN_COLS = CONV_OFF + 2 * CONV_CH

kernel_name = "hymba_style_hybrid_encoder_block"


def rms_norm(x, g):
    xf = x.astype(jnp.float32)
    y = xf * lax.rsqrt(jnp.mean(xf * xf, axis=-1, keepdims=True) + EPS)
    return (y * g.astype(jnp.float32)).astype(x.dtype)


def layer_norm(x, g, b):
    xf = x.astype(jnp.float32)
    mu = jnp.mean(xf, axis=-1, keepdims=True)
    xc = xf - mu
    y = xc * lax.rsqrt(jnp.mean(xc * xc, axis=-1, keepdims=True) + EPS)
    return (y * g.astype(jnp.float32) + b.astype(jnp.float32)).astype(x.dtype)


def rope_tables(positions):
    inv_freq = ROPE_THETA ** (-jnp.arange(0, ROT_DIM, 2, dtype=jnp.float32) / ROT_DIM)
    ang = positions.astype(jnp.float32)[..., None] * inv_freq
    return jnp.cos(ang)[:, :, None, :], jnp.sin(ang)[:, :, None, :]


def apply_partial_rope(x, cos, sin):
    xr = x[..., :ROT_DIM].astype(jnp.float32)
    half = ROT_DIM // 2
    x1, x2 = xr[..., :half], xr[..., half:]
    rot = jnp.concatenate([x1 * cos - x2 * sin, x2 * cos + x1 * sin], axis=-1)
    return jnp.concatenate([rot.astype(x.dtype), x[..., ROT_DIM:]], axis=-1)


def band_mask(n_blocks, seq):
    qi = jnp.arange(BLOCK)[:, None]
    kj = jnp.arange(3 * BLOCK)[None, :]
    rel = kj - BLOCK - qi
    in_band = jnp.abs(rel) <= WINDOW
    key_abs = jnp.arange(n_blocks)[:, None, None] * BLOCK - BLOCK + kj[None]
    in_range = (key_abs >= 0) & (key_abs < seq)
    return in_band[None] & in_range


def windowed_gqa_with_sink(q, k, v, sink):
    B, S = q.shape[0], q.shape[1]
    nb = S // BLOCK
    qb = q.reshape(B, nb, BLOCK, ATTN_KV_HEADS, ATTN_GROUP, HEAD_DIM)
    pad = ((0, 0), (BLOCK, BLOCK), (0, 0), (0, 0))
    kp = jnp.pad(k, pad).reshape(B, nb + 2, BLOCK, ATTN_KV_HEADS, HEAD_DIM)
    vp = jnp.pad(v, pad).reshape(B, nb + 2, BLOCK, ATTN_KV_HEADS, HEAD_DIM)
    kw = jnp.concatenate([kp[:, :-2], kp[:, 1:-1], kp[:, 2:]], axis=2)
    vw = jnp.concatenate([vp[:, :-2], vp[:, 1:-1], vp[:, 2:]], axis=2)
    s = jnp.einsum('bnqhgd,bnkhd->bnhgqk', qb, kw,
                   preferred_element_type=jnp.float32) * (1.0 / math.sqrt(HEAD_DIM))
    mask = band_mask(nb, S)[None, :, None, None]
    s = jnp.where(mask, s, NEG_INF)
    sink_l = sink.astype(jnp.float32).reshape(ATTN_KV_HEADS, ATTN_GROUP)[None, None, :, :, None, None]
    m = jnp.maximum(jnp.max(s, axis=-1, keepdims=True), sink_l)
    e = jnp.exp(s - m)
    denom = jnp.sum(e, axis=-1, keepdims=True) + jnp.exp(sink_l - m)
    pr = (e / denom).astype(v.dtype)
    o = jnp.einsum('bnhgqk,bnkhd->bnqhgd', pr, vw)
    return o.reshape(B, S, ATTN_WIDTH)


def spatial_gating(uv, ln_g, ln_b, ws, bs):
    B, S = uv.shape[0], uv.shape[1]
    u, v = uv[..., :GM_WIDTH], uv[..., GM_WIDTH:]
    v = layer_norm(v, ln_g, ln_b)
    vb = v.reshape(B, S // CHUNK, CHUNK, GM_HEADS, HEAD_DIM)
    sg = jnp.einsum('hpq,bnqhd->bnphd', ws, vb) + bs.T[None, None, :, :, None]
    return u * sg.reshape(B, S, GM_WIDTH)


def conformer_conv(ag, conv_w, conv_b, ln_g, ln_b):
    a, g = ag[..., :CONV_CH], ag[..., CONV_CH:]
    glu = a * jax.nn.sigmoid(g)
    y = lax.conv_general_dilated(glu, conv_w[:, None, :], window_strides=(1,),
                                 padding=[(CONV_PAD, CONV_PAD)],
                                 dimension_numbers=('NWC', 'WIO', 'NWC'),
                                 feature_group_count=CONV_CH) + conv_b
    y = layer_norm(y, ln_g, ln_b)
    return jax.nn.silu(y)


def setup_inputs(seed: int = 0) -> dict:
    key = jax.random.key(seed)
    ks = jax.random.split(key, 24)
    f32 = jnp.float32

    def nrm(k, shape, scale):
        return jax.random.normal(k, shape, f32) * scale

    def gain(k, shape):
        return 1.0 + 0.02 * jax.random.normal(k, shape, f32)

    x = jax.random.normal(ks[0], (BATCH, SEQ, D_MODEL), f32)
    p = jax.random.normal(ks[1], (DEPTH, BATCH, SEQ, PLE_DIM), f32)
    positions = jnp.broadcast_to(jnp.arange(SEQ, dtype=jnp.int32), (BATCH, SEQ))
    return {
        "x": x,
        "p": p,
        "positions": positions,
        "norm_mix_g": gain(ks[2], (DEPTH, D_MODEL)),
        "w_in": nrm(ks[3], (DEPTH, D_MODEL, IN_COLS), D_MODEL ** -0.5),
        "q_norm_g": gain(ks[4], (DEPTH, HEAD_DIM)),
        "k_norm_g": gain(ks[5], (DEPTH, HEAD_DIM)),
        "sink": nrm(ks[6], (DEPTH, ATTN_Q_HEADS), 0.5),
        "gm_ln_g": gain(ks[7], (DEPTH, GM_WIDTH)),
        "gm_ln_b": nrm(ks[8], (DEPTH, GM_WIDTH), 0.02),
        "gm_ws": nrm(ks[9], (DEPTH, GM_HEADS, CHUNK, CHUNK), CHUNK ** -0.5),
        "gm_bs": gain(ks[10], (DEPTH, GM_HEADS, CHUNK)),
        "conv_w": nrm(ks[11], (DEPTH, CONV_WIDTH, CONV_CH), CONV_WIDTH ** -0.5),
        "conv_b": nrm(ks[12], (DEPTH, CONV_CH), 0.02),
        "conv_ln_g": gain(ks[13], (DEPTH, CONV_CH)),
        "conv_ln_b": nrm(ks[14], (DEPTH, CONV_CH), 0.02),
        "out_norm_g": gain(ks[15], (DEPTH, D_MIX)),
        "w_out": nrm(ks[16], (DEPTH, D_MIX, D_MODEL), D_MIX ** -0.5),
        "norm_ffn_g": gain(ks[17], (DEPTH, D_MODEL)),
        "w_gate_up": nrm(ks[18], (DEPTH, D_MODEL, 2 * D_FF), D_MODEL ** -0.5),
        "w_down": nrm(ks[19], (DEPTH, D_FF, D_MODEL), D_FF ** -0.5),
        "ple_norm_g": gain(ks[20], (DEPTH, D_MODEL)),
        "w_ple_gate": nrm(ks[21], (DEPTH, D_MODEL, D_MODEL), D_MODEL ** -0.5),
        "w_ple_proj": nrm(ks[22], (DEPTH, PLE_DIM, D_MODEL), PLE_DIM ** -0.5),
    }


def reference(x, p, positions, norm_mix_g, w_in, q_norm_g, k_norm_g, sink,
              gm_ln_g, gm_ln_b, gm_ws, gm_bs, conv_w, conv_b, conv_ln_g, conv_ln_b,
              out_norm_g, w_out, norm_ffn_g, w_gate_up, w_down,
              ple_norm_g, w_ple_gate, w_ple_proj):
    B, S = x.shape[0], x.shape[1]
    cos, sin = rope_tables(positions)
    for i in range(DEPTH):
        h = rms_norm(x, norm_mix_g[i])
        z = h @ w_in[i]

        q = z[..., Q_OFF:K_OFF].reshape(B, S, ATTN_Q_HEADS, HEAD_DIM)
        k = z[..., K_OFF:V_OFF].reshape(B, S, ATTN_KV_HEADS, HEAD_DIM)
        v = z[..., V_OFF:GM_OFF].reshape(B, S, ATTN_KV_HEADS, HEAD_DIM)
        q = apply_partial_rope(rms_norm(q, q_norm_g[i]), cos, sin)
        k = apply_partial_rope(rms_norm(k, k_norm_g[i]), cos, sin)
        o_attn = windowed_gqa_with_sink(q, k, v, sink[i])

        uv = jax.nn.gelu(z[..., GM_OFF:CONV_OFF], approximate=False)
        o_gm = spatial_gating(uv, gm_ln_g[i], gm_ln_b[i], gm_ws[i], gm_bs[i])

        o_conv = conformer_conv(z[..., CONV_OFF:IN_COLS], conv_w[i], conv_b[i],
                                conv_ln_g[i], conv_ln_b[i])

        g_out = out_norm_g[i]
        merged = jnp.concatenate([
            rms_norm(o_attn, g_out[:ATTN_WIDTH]),
            rms_norm(o_gm, g_out[ATTN_WIDTH:ATTN_WIDTH + GM_WIDTH]),
            rms_norm(o_conv, g_out[ATTN_WIDTH + GM_WIDTH:]),
        ], axis=-1)
        x = x + merged @ w_out[i]

        hf = rms_norm(x, norm_ffn_g[i]) @ w_gate_up[i]
        x = x + (jax.nn.silu(hf[..., :D_FF]) * hf[..., D_FF:]) @ w_down[i]

        gate = jax.nn.sigmoid(rms_norm(x, ple_norm_g[i]) @ w_ple_gate[i])
        x = x + (p[i] @ w_ple_proj[i]) * gate
    return x
```

```python
import math
from contextlib import ExitStack
import numpy as np
import concourse.bass as bass
import concourse.mybir as mybir
from concourse.bass_utils import run_bass_kernel_spmd

F32 = mybir.dt.float32; BF16 = mybir.dt.bfloat16; I32 = mybir.dt.int32
AF = mybir.ActivationFunctionType; ALU = mybir.AluOpType; AX = mybir.AxisListType

NCORES = 8
D = 1024; S = 4096; DEPTH = 2; T = 512; NG = S // T
DFF = 2816; INC = 1792; PLE = 256
EPS = 1e-6
NSLOT = 8
CELL = 64


def _box(ap):
    es = mybir.dt.size(ap.dtype)
    a = ap.ap
    ps = a[0][0]
    off = ap.offset
    fo = off % ps if ps > 0 else off
    ext = 0
    for (st, cn) in a[1:]:
        ext += abs(st) * (cn - 1)
    lo = fo * es
    hi = (fo + ext + 1) * es
    if ap.tensor.name.startswith('bk'):
        return (ap.tensor.name, 0, 0)
    return (ap.tensor.name, lo // CELL, (hi - 1) // CELL)


class Sched:
    ENG = ['pe', 'act', 'dve', 'pool', 'sp']

    def __init__(self, nc, sems):
        self.nc = nc
        self.sem = sems
        self.cnt = {k: 0 for k in sems}
        self.prog = {e: [] for e in self.ENG}
        self.seen = {e: {} for e in self.ENG}
        self.cells = {}
        self.last = {}
        self.dead = False
        self.nrec = 0
        self.maxrec = 10**9
        self.log = []

    def _deps(self, reads, writes):
        deps = {}
        for ap in reads:
            n, lo, hi = _box(ap)
            for c in range(lo, hi + 1):
                st = self.cells.get((n, c))
                if st and st[0]:
                    s, v = st[0]
                    if deps.get(s, 0) < v: deps[s] = v
        for ap in writes:
            n, lo, hi = _box(ap)
            for c in range(lo, hi + 1):
                st = self.cells.get((n, c))
                if st:
                    if st[0]:
                        s, v = st[0]
                        if deps.get(s, 0) < v: deps[s] = v
                    for s, v in st[1].items():
                        if deps.get(s, 0) < v: deps[s] = v
        return deps

    def _emit_waits(self, eng, deps):
        for s, v in deps.items():
            if s == 'pe' and eng == 'pe': continue
            if self.seen[eng].get(s, 0) >= v: continue
            self.seen[eng][s] = v
            self.prog[eng].append(('wait', s, v))

    def _mark(self, reads, writes, tick):
        s, v = tick
        for ap in reads:
            n, lo, hi = _box(ap)
            for c in range(lo, hi + 1):
                st = self.cells.setdefault((n, c), [None, {}])
                if st[1].get(s, 0) < v: st[1][s] = v
        for ap in writes:
            n, lo, hi = _box(ap)
            for c in range(lo, hi + 1):
                self.cells[(n, c)] = [tick, {}]

    def op(self, eng, fn, r=(), w=(), extra=()):
        if self.dead: return ('pe', 0)
        self.nrec += 1
        if self.nrec > self.maxrec: return ('pe', 0)
        import traceback
        self.log.append((self.nrec, eng, traceback.extract_stack()[-3].lineno, traceback.extract_stack()[-2].lineno))
        deps = self._deps(r, w)
        for sv in extra:
            if sv is not None and deps.get(sv[0], 0) < sv[1]: deps[sv[0]] = sv[1]
        self._emit_waits(eng, deps)
        self.cnt[eng] += 1
        tick = (eng, self.cnt[eng])
        self.prog[eng].append(('ins', fn, eng))
        self._mark(r, w, tick)
        return tick

    def dma(self, q, semname, fn, r=(), w=(), extra=()):
        if self.dead: return ('pe', 0)
        deps = self._deps(r, w)
        for sv in list(extra) + [self.last.get(semname)]:
            if sv is not None and deps.get(sv[0], 0) < sv[1]: deps[sv[0]] = sv[1]
        self._emit_waits(q, deps)
        self.cnt[semname] += 16
        tick = (semname, self.cnt[semname])
        self.last[semname] = tick
        self.prog[q].append(('dma', fn, semname))
        self._mark(r, w, tick)
        return tick

    def wait(self, eng, tick):
        self._emit_waits(eng, {tick[0]: tick[1]})

    def replay(self, block):
        engs = {'pe': block.tensor, 'act': block.scalar, 'dve': block.vector,
                'pool': block.gpsimd, 'sp': block.sync}
        for en, deco in engs.items():
            prog = self.prog[en]

            def body(e, prog=prog):
                for it in prog:
                    if it[0] == 'wait':
                        e.wait_ge(self.sem[it[1]], it[2])
                    elif it[0] == 'ins':
                        it[1](e).then_inc(self.sem[it[2]], 1)
                    else:
                        it[1](e, self.sem[it[2]])
            deco(body)


def slab_table():
    t = []
    for s in range(2): t.append(('w_in', 4 * s, 4, 512, 256))
    for c in range(4): t.append(('w_in', 0, 8, 1280 + 128 * c, 128))
    for s in range(2): t.append(('w_in', 4 * s, 4, 1024, 256))
    for c in range(2): t.append(('w_in', 0, 8, 768 + 128 * c, 128))
    for s in range(4): t.append(('w_in', 2 * s, 2, 0, 512))
    for m in range(8): t.append(('w_out', 0, 8, 128 * m, 128))
    for (f0, f1) in ((0, 16), (16, 22)):
        for f in range(f0, f1):
            t.append(('w_gate_up', 0, 8, 128 * f, 128))
            t.append(('w_gate_up', 0, 8, DFF + 128 * f, 128))
        for m in range(8):
            for k0 in range(f0, f1, 8):
                t.append(('w_down', k0, min(8, f1 - k0), 128 * m, 128))
    for s in range(2):
        t.append(('w_ple_proj', 0, 2, 512 * s, 512))
        for m in range(4 * s, 4 * s + 4): t.append(('w_ple_gate', 0, 8, 128 * m, 128))
    return t


NPC = 104
PC_MIX, PC_FFN, PC_PLE, PC_OGA, PC_OGG, PC_OGC, PC_CW, PC_CB, PC_CLG, PC_CLB = 0, 8, 16, 24, 28, 30, 32, 94, 96, 98
NPR = 648
PR_QG, PR_KG, PR_SINK, PR_LNG, PR_LNB = 0, 64, 128, 136, 392


def build_nc(layers, last, ngrun=NG, stage=99):
    import os as _os2
    nc = bass.Bass("TRN2", target_bir_lowering=False, dynamic_dma_scratch_size=int(_os2.environ.get("KSCR", 4096)))
    dram = {}
    def din(name, shape, dt=F32):
        dram[name] = nc.dram_tensor(name, shape, dt, kind="ExternalInput").ap()
        return dram[name]
    xT = din("xT", [D, S]); pTd = din("pT", [DEPTH, PLE, S]); posd = din("pos", [128, 32], I32)
    constd = din("consts", [128, 392]); pcolsd = din("pcols", [DEPTH, 128, NPC]); prowd = din("prow", [DEPTH, 1, NPR])
    bsd = din("gm_bs", [DEPTH, 4, 128]); wsd = din("wsT", [DEPTH, 128, 512])
    wd = {}
    for name, shp in (("w_in", [DEPTH, D, INC]), ("w_out", [DEPTH, D, D]), ("w_gate_up", [DEPTH, D, 2 * DFF]),
                      ("w_down", [DEPTH, DFF, D]), ("w_ple_gate", [DEPTH, D, D]), ("w_ple_proj", [DEPTH, PLE, D])):
        wd[name] = din(name, shp)
    oT = nc.dram_tensor("oT", [D, S], F32, kind="ExternalOutput").ap()
    slabs = slab_table(); NS = len(slabs)
    scr = nc.dram_tensor("wscr", [DEPTH * NS, 128, 1024], BF16, kind="Internal").ap()

    with ExitStack() as es:
        X = es.enter_context(nc.sbuf_tensor("X", [128, 8, S], F32))
        ARENA_F = (81664 + 16384 - int(_os2.environ.get('KSCR', 4096))) // 4
        A = es.enter_context(nc.sbuf_tensor("A", [128, ARENA_F], F32))
        banks = [es.enter_context(nc.psum_tensor(f"bk{i}", [128, 512], F32)) for i in range(8)]
        semnames = ['pe', 'act', 'dve', 'pool', 'sp', 'xl0', 'xl1', 'par', 'pl', 'st', 'out'] + \
                   [f'ws{i}' for i in range(NSLOT)] + [f'wq{i}' for i in range(NSLOT)] + [f'wst{i}' for i in range(4)] + [f'p{i}' for i in range(10)]
        sems = {n: es.enter_context(nc.semaphore(n)) for n in semnames}
        block = es.enter_context(nc.Block())
        SC = Sched(nc, sems)
        import os
        SC.maxrec = int(os.environ.get('KMAX', 10**9))
        build_nc.SC = SC

        cur = [0]
        def carve(nbytes, dt=F32, shape=None, at=None):
            if at is None:
                off = cur[0]; cur[0] += (nbytes + 63) // 64 * 64
            else:
                off = at
            assert off % 4 == 0 and off + nbytes <= ARENA_F * 4, (off, nbytes)
            v = A[:, off // 4:(off + nbytes + 3) // 4]
            if dt != F32: v = v.bitcast(dt)
            return v, off
        def view3(v, pat, **kw): return v.rearrange(pat, **kw)

        hT_f, _ = carve(8 * 640 * 2, BF16); hT = hT_f.rearrange("p (k t) -> p k t", k=8)
        ring = [carve(2048, BF16)[0] for _ in range(NSLOT)]
        mT_f, _ = carve(8 * 512 * 2, BF16); mergedT = mT_f.rearrange("p (k t) -> p k t", k=8)
        rstd_b, _ = carve(640 * 4)
        sq_f, _ = carve(2 * 640 * 2, BF16); sqb = sq_f.rearrange("p (k t) -> p k t", k=2)
        ident, _ = carve(256, BF16); ones, _ = carve(256, BF16)
        negp, _ = carve(1024, BF16); negn, _ = carve(1024, BF16)
        cos_f, _ = carve(1024); sin_f, _ = carve(1024)
        cosT = cos_f.rearrange("p (n j) -> p n j", j=8); sinT = sin_f.rearrange("p (n j) -> p n j", j=8)
        mhalf, _ = carve(64)
        pcols, _ = carve(NPC * 4); prow, _ = carve(NPR * 4)
        bsT_f, _ = carve(1024); bsT = bsT_f.rearrange("p (c q) -> p c q", c=2)
        wsT_f, _ = carve(1024, BF16); wsT = wsT_f.rearrange("p (h q) -> p h q", h=4)
        esink, _ = carve(64)
        kT_f, _ = carve(4096, BF16); kT = kT_f.rearrange("p (h s t) -> p h s t", h=2, s=8)
        va_f, _ = carve(8 * 2 * 65 * 2 + 32, BF16); vaug = va_f[:, 0:8 * 2 * 65].rearrange("p (s h d) -> p s h d", s=8, h=2)
        gtail_f, _ = carve(256); gtail = gtail_f.rearrange("p (c j) -> p c j", c=2)
        stat, _ = carve(512)
        U0 = cur[0]
        USZ = ARENA_F * 4 - U0
        assert USZ >= 36096, USZ
        def ucarve(off, nbytes, dt=F32): return carve(nbytes, dt, at=U0 + off)[0]
        rstd_n = ucarve(22528, 640 * 4)
        pTb_f = ucarve(20480, 2048, BF16); pTb = pTb_f.rearrange("p (c t) -> p c t", c=2)
        act_f = ucarve(0, 16 * 512 * 2, BF16); actb = act_f.rearrange("p (k t) -> p k t", k=16)
        sg2 = [ucarve(16384, 2048), ucarve(18432, 2048)]
        ptmp = [ucarve(0, 2048), ucarve(2048, 2048)]
        q_fs = [ucarve(0, 2048), ucarve(2048, 2048)]
        sqtmps = [ucarve(4096, 2048), ucarve(18432, 2048)]
        q_bfs = [ucarve(6144, 1024, BF16), ucarve(7168, 1024, BF16)]
        qT_sbs = [ucarve(8192, 2048, BF16), ucarve(10240, 2048, BF16)]
        PT_f = ucarve(12288, 6144, BF16); PT = PT_f.rearrange("p (i t) -> p i t", i=6)
        sqtmps[1] = ucarve(20480, 2048)
        o_f = ucarve(18432, 2048)
        k_fs = [ucarve(12288, 512), ucarve(12800, 512)]; k_bfs = [ucarve(13312, 256, BF16), ucarve(13568, 256, BF16)]
        k_sq = [ucarve(13824, 512), ucarve(14336, 512)]; rtmps = [ucarve(14848, 256), ucarve(15104, 256)]
        m_bf = ucarve(35072, 1024, BF16)
        glu_f = ucarve(22528, 4352); glu = glu_f.rearrange("p (c j) -> p c j", c=2)
        sig = ucarve(26880, 2048); acc_f = ucarve(28928, 4096); acc = acc_f.rearrange("p (c t) -> p c t", c=2)
        ybf_f = ucarve(33024, 2048, BF16); ybf = ybf_f.rearrange("p (c t) -> p c t", c=2)
        gvs = [ucarve(0, 1024), ucarve(1024, 1024)]; vn_bfs = [ucarve(2048, 512, BF16), ucarve(2560, 512, BF16)]
        uT_f = ucarve(3072, 4096); uT = uT_f.rearrange("p (c t) -> p c t", c=2)
        gtmps = [ucarve(7168, 1024), ucarve(8192, 1024)]

        freeb = list(range(8))
        def bank():
            assert freeb, "no free PSUM bank"
            return banks[freeb.pop(0)][:]
        def rel(*aps):
            for ap in aps:
                i = int(ap.tensor.name[2:])
                assert i not in freeb
                freeb.append(i)

        def ACT(out, in_, func, r=None, w=None, **kw):
            return SC.op('act', lambda e: e.activation(out=out, in_=in_, func=func, **kw),
                         r=(r if r is not None else [in_]), w=(w if w is not None else [out]))
        def TT(eng, out, in0, in1, op, r=None, w=None):
            return SC.op(eng, lambda e: e.tensor_tensor(out=out, in0=in0, in1=in1, op=op),
                         r=(r if r is not None else [in0, in1]), w=(w if w is not None else [out]))
        def TS(eng, out, in0, s1, s2, op0, op1=None, r=None, w=None):
            rr = [in0] + [s for s in (s1, s2) if not isinstance(s, (int, float, type(None)))]
            if op1 is None:
                return SC.op(eng, lambda e: e.tensor_scalar(out=out, in0=in0, scalar1=s1, scalar2=None, op0=op0),
                             r=(r if r is not None else rr), w=(w if w is not None else [out]))
            return SC.op(eng, lambda e: e.tensor_scalar(out=out, in0=in0, scalar1=s1, scalar2=s2, op0=op0, op1=op1),
                         r=(r if r is not None else rr), w=(w if w is not None else [out]))
        def STT(out, in0, scalar, in1, op0, op1, r=None, w=None):
            rr = [in0, in1] + ([] if isinstance(scalar, (int, float)) else [scalar])
            return SC.op('dve', lambda e: e.scalar_tensor_tensor(out=out, in0=in0, scalar=scalar, in1=in1, op0=op0, op1=op1),
                         r=(r if r is not None else rr), w=(w if w is not None else [out]))
        def COPY(eng, out, in_):
            if eng == 'act':
                return ACT(out, in_, AF.Copy)
            return SC.op(eng, lambda e: e.tensor_copy(out=out, in_=in_), r=[in_], w=[out])
        def MEMSET(eng, ap, val):
            return SC.op(eng, lambda e: e.memset(ap, val), w=[ap])
        def MM(out, pairs, r=None, first=True, last=True, split=False):
            if split and len(pairs) > 1:
                for i, pr in enumerate(pairs):
                    MM(out, [pr], first=(first and i == 0), last=(last and i == len(pairs) - 1))
                return
            def fn(e):
                n = len(pairs)
                for i, (l, rh) in enumerate(pairs):
                    ins = e.matmul(out, lhsT=l, rhs=rh, start=(first and i == 0), stop=(last and i == n - 1))
                return ins
            rr = r if r is not None else [a for pr in pairs for a in pr]
            return SC.op('pe', fn, r=rr, w=[out])
        def TRANS(out_list, in_list, r, w):
            def fn(e):
                for o, i in zip(out_list, in_list):
                    ins = e.transpose(out=o, in_=i, identity=ident)
                return ins
            return SC.op('pe', fn, r=list(r) + [ident], w=w)
        def POW(ap):
            shp = list(ap.shape)
            return TT('pool', ap, ap, mhalf[:, 0:1].to_broadcast(shp) if len(shp) == 2 else mhalf[:, 0:1].unsqueeze(2).to_broadcast(shp), ALU.pow,
                      r=[ap, mhalf[:, 0:1]], w=[ap])

        epsc = mhalf
        def RSQ(out, in_, scale):
            ACT(out, in_, AF.Ln, scale=scale, bias=epsc[:, 0:1], r=[in_, epsc[:, 0:1]])
            ACT(out, out, AF.Exp, scale=-0.5)

        wstate = {'next': 0, 'store': {}}
        def w_src(l, i):
            name, kc0, nk, c0, ncols = slabs[i]
            Wv = wd[name][l].rearrange("(kc p) c -> p kc c", p=128)
            return Wv[:, kc0:kc0 + nk, c0:c0 + ncols], nk, ncols
        def w_issue(gidx):
            li, rem = divmod(gidx, NG * NS)
            if li >= len(layers): return
            l = layers[li]
            it, i = divmod(rem, NS)
            slot = gidx % NSLOT
            src, nk, ncols = w_src(l, i)
            dst = ring[slot][:, 0:nk * ncols].rearrange("p (k c) -> p k c", k=nk)
            if it == 0:
                SC.dma('pool', f'wq{slot}', lambda e, s: e.dma_start(out=dst, in_=src).then_inc(s, 16), w=[ring[slot]])
                sidx = l * NS + i
                stsem = f'wst{gidx % 4}'
                tk = SC.dma('sp', stsem, lambda e, s: e.dma_start(out=scr[sidx], in_=ring[slot]).then_inc(s, 16), r=[ring[slot]])
                wstate['store'][(l, i)] = tk
            else:
                sidx = l * NS + i
                SC.dma('sp', f'ws{slot}', lambda e, s: e.dma_start(out=ring[slot], in_=scr[sidx]).then_inc(s, 16),
                       w=[ring[slot]], extra=[wstate['store'].get((l, i))])
        wctr = [0]
        held = [None]
        def WHOLD(): held[0] = wctr[0]
        def WRELEASE(): held[0] = None
        def WGET(expect):
            g = wctr[0]; wctr[0] += 1
            base = g if held[0] is None else held[0]
            while wstate['next'] <= base + NSLOT - 1:
                w_issue(wstate['next']); wstate['next'] += 1
            i = g % NS
            name, kc0, nk, c0, ncols = slabs[i]
            assert (name, kc0, c0) == expect, (slabs[i], expect)
            slot = g % NSLOT
            return ring[slot][:, 0:nk * ncols].rearrange("p (k c) -> p k c", k=nk), ring[slot]

        xTv = xT.rearrange("(kc p) t -> p kc t", p=128)
        oTv = oT.rearrange("(kc p) t -> p kc t", p=128)
        def load_x(g):
            SC.dma('sp', f'xl{g % 2}', lambda e, s: e.dma_start(out=X[:, :, g * T:(g + 1) * T], in_=xTv[:, :, g * T:(g + 1) * T]).then_inc(s, 16),
                   w=[X[:, kc, g * T:(g + 1) * T] for kc in range(8)])
        load_x(0)
        if ngrun > 1: load_x(1)
        cst = ucarve(8192, 392 * 4)
        SC.dma('sp', 'p0', lambda e, s: e.dma_start(out=cst, in_=constd[:, :]).then_inc(s, 16), w=[cst])
        posi = ucarve(9792, 128, I32); posf = ucarve(9920, 128)
        SC.dma('sp', 'p1', lambda e, s: e.dma_start(out=posi, in_=posd[:, :]).then_inc(s, 16), w=[posi])
        COPY('dve', ident, cst[:, 0:128])
        for g4 in range(4):
            TS('dve', negp[:, g4 * 128:(g4 + 1) * 128], cst[:, 128:256], -1.0, 30000.0, ALU.add, ALU.mult)
            TS('dve', negn[:, g4 * 128:(g4 + 1) * 128], cst[:, 256:384], -1.0, 30000.0, ALU.add, ALU.mult)
        MEMSET('dve', ones, 1.0); MEMSET('dve', epsc, EPS)
        MEMSET('dve', va_f, 1.0)
        MEMSET('dve', stat, 0.0)
        COPY('dve', posf, posi)
        def setup_rope():
            angt = ucarve(0, 1024); ang3 = angt.rearrange("p (n j) -> p n j", j=8)
            ut = ucarve(1024, 1024); ki = ucarve(2048, 1024, I32); kf = ucarve(3072, 1024)
            TT('dve', ang3, posf.unsqueeze(2).to_broadcast([128, 32, 8]), cst[:, 384:392].unsqueeze(1).to_broadcast([128, 32, 8]), ALU.mult,
               r=[posf, cst[:, 384:392]], w=[angt])
            for (dstT, shift) in ((sin_f, 0.0), (cos_f, 0.25)):
                TS('dve', ut, angt, 1.0 / (2 * math.pi), shift, ALU.mult, ALU.add)
                COPY('dve', ki, ut); COPY('dve', kf, ki)
                TT('dve', ut, ut, kf, ALU.subtract)
                TS('dve', ut, ut, 2 * math.pi, 3.14159, ALU.mult, ALU.min)
                TS('dve', ut, ut, -3.14159, None, ALU.max)
                ACT(dstT, ut, AF.Sin)


        for li, l in enumerate(layers):
            SC.dma('sp', 'p2', lambda e, s, l=l: e.dma_start(out=pcols, in_=pcolsd[l]).then_inc(s, 16), w=[pcols])
            SC.dma('sp', 'p3', lambda e, s, l=l: e.dma_start(out=prow, in_=prowd[l, 0, :].partition_broadcast(128)).then_inc(s, 16), w=[prow])
            for h in range(4):
                c, j = divmod(h, 2)
                SC.dma('sp', f'p{4 + h}', lambda e, s, l=l, h=h, c=c, j=j: e.dma_start(out=bsT[64 * j:64 * j + 64, c, :], in_=bsd[l, h, :].partition_broadcast(64)).then_inc(s, 16),
                       w=[bsT[:, c, :]])
            wstg = ucarve(0, 2048)
            SC.dma('sp', 'p8', lambda e, s, l=l: e.dma_start(out=wstg, in_=wsd[l]).then_inc(s, 16), w=[wstg])
            COPY('dve', wsT_f, wstg)
            ACT(esink[:, 0:8], prow[:, PR_SINK:PR_SINK + 8], AF.Exp)

            for it in range(ngrun):
                if li == 0 and it == 0: prestat = [False]
                t0 = it * T
                if stage == 0: SC.dead = True
                if li == 0 and it + 2 < ngrun:
                    load_x(it + 2)
                W = min(640, S - t0)

                def norm_pieces(Wn): return [(0, min(512, Wn))] + ([(512, Wn)] if Wn > 512 else [])
                def norm_stats(kc, Wn, tcol, bks, pieces):
                    ACT(sqb[:, kc % 2, 0:Wn], X[:, kc, tcol:tcol + Wn], AF.Square)
                    for (c0, c1), bk in zip(pieces, bks):
                        MM(bk[:, 0:c1 - c0], [(ones, sqb[:, kc % 2, c0:c1])], first=(kc == 0), last=(kc == 7))
                def norm_fin(bks, pieces, rbuf):
                    for (c0, c1), bk in zip(pieces, bks):
                        RSQ(rbuf[:, c0:c1], bk[:, 0:c1 - c0], 1.0 / D)
                    rel(*bks)
                def norm_apply(pc0, Wn, tcol, rbuf):
                    for kc in range(8):
                        STT(hT[:, kc, 0:Wn], X[:, kc, tcol:tcol + Wn], pcols[:, pc0 + kc:pc0 + kc + 1], rbuf[:, 0:Wn], ALU.mult, ALU.mult)

                if not prestat[0]:
                    pcs = norm_pieces(W); bks = [bank() for _ in pcs]
                    for kc in range(8): norm_stats(kc, W, t0, bks, pcs)
                    norm_fin(bks, pcs, rstd_n)
                norm_apply(PC_MIX, W, t0, rstd_n)
                prestat[0] = False
                if li == 0 and it == 0: setup_rope()
                if it + 1 < ngrun: nxt_t0 = (it + 1) * T
                elif li + 1 < len(layers): nxt_t0 = 0
                else: nxt_t0 = None

                if stage == 1: SC.dead = True
                kvt = [j for j in range(0 if it == 0 else 1, 5) if t0 + j * 128 < S]
                kvb = {}; kvbanks = []
                for idx, j in enumerate(kvt):
                    if idx % 2 == 0:
                        bcur = bank(); kvbanks.append(bcur)
                    kvb[j] = bcur[:, (idx % 2) * 256:(idx % 2) * 256 + 256]
                WHOLD()
                wkv = [WGET(('w_in', 4 * s2, 512))[0] for s2 in range(2)]
                for j in kvt:
                    MM(kvb[j], [(hT[:, 4 * s2 + kk, j * 128:(j + 1) * 128], wkv[s2][:, kk, :]) for s2 in range(2) for kk in range(4)], split=(j == kvt[0]))
                WRELEASE()

                def merge(*gens):
                    active = [[g, n] for g, n in gens]
                    while active:
                        for ent in list(active):
                            for _ in range(ent[1]):
                                try:
                                    next(ent[0])
                                except StopIteration:
                                    active.remove(ent); break

                def qk_prep(src_ps, nh, gofs, xfb, bfbuf, gtile, sqt, ss, rt):
                    n = nh * 64
                    xf = xfb[:, 0:n]
                    COPY('act', xf, src_ps); yield
                    TT('pool', sqt[:, 0:n], xf, xf, ALU.mult); yield
                    SC.op('dve', lambda e: e.tensor_reduce(out=ss, in_=sqt[:, 0:n].rearrange("p (h d) -> p h d", h=nh), axis=AX.X, op=ALU.add),
                          r=[sqt[:, 0:n]], w=[ss]); yield
                    ACT(ss, ss, AF.Ln, scale=1.0 / 64, bias=epsc[:, 0:1], r=[ss, epsc[:, 0:1]]); yield
                    ACT(ss, ss, AF.Exp, scale=-0.5); yield
                    x3 = xf.rearrange("p (h d) -> p h d", h=nh)
                    TT('dve', x3, x3, ss.unsqueeze(2).to_broadcast([128, nh, 64]), ALU.mult, r=[xf, ss], w=[xf]); yield
                    TT('dve', x3, x3, prow[:, gofs:gofs + 64].unsqueeze(1).to_broadcast([128, nh, 64]), ALU.mult,
                       r=[xf, prow[:, gofs:gofs + 64]], w=[xf]); yield
                    x1 = x3[:, :, 0:8]; x2 = x3[:, :, 8:16]
                    cb = cosT[:, gtile, :].unsqueeze(1).to_broadcast([128, nh, 8])
                    sb_ = sinT[:, gtile, :].unsqueeze(1).to_broadcast([128, nh, 8])
                    w8 = nh * 8
                    tt = [rt[:, i * w8:(i + 1) * w8].rearrange("p (h d) -> p h d", h=nh) for i in range(4)]
                    tr_ = [rt[:, 0:4 * w8]]
                    cr = [cosT[:, gtile, :]]; sr = [sinT[:, gtile, :]]
                    TT('pool', tt[0], x1, cb, ALU.mult, r=[xf] + cr, w=tr_); yield
                    TT('pool', tt[1], x2, sb_, ALU.mult, r=[xf] + sr, w=tr_); yield
                    TT('pool', tt[2], x2, cb, ALU.mult, r=[xf] + cr, w=tr_); yield
                    TT('pool', tt[3], x1, sb_, ALU.mult, r=[xf] + sr, w=tr_); yield
                    TT('pool', x1, tt[0], tt[1], ALU.subtract, r=tr_, w=[xf]); yield
                    TT('pool', x2, tt[2], tt[3], ALU.add, r=tr_, w=[xf]); yield
                    COPY('act', bfbuf[:, 0:n], xf); yield

                def kv_tile(j, par):
                    gt = 4 * it + j
                    slot = gt % 8
                    yield from qk_prep(kvb[j][:, 0:128], 2, PR_KG, k_fs[par], k_bfs[par], gt, k_sq[par], stat[:, 2 * par:2 * par + 2], rtmps[par])
                    tb = bank(); tbb = tb[:].bitcast(BF16)
                    TRANS([tbb[0:64, h * 128:(h + 1) * 128] for h in range(2)], [k_bfs[par][:, h * 64:(h + 1) * 64] for h in range(2)],
                          r=[k_bfs[par]], w=[tbb[0:64, 0:256]]); yield
                    SC.op('dve', lambda e, tbb=tbb, slot=slot: e.tensor_copy(out=kT[0:64, :, slot, :], in_=tbb[0:64, 0:256].rearrange("p (h t) -> p h t", h=2)),
                          r=[tbb[0:64, 0:256]], w=[kT[0:64, h, slot, :] for h in range(2)]); rel(tb); yield
                    COPY('act', vaug[:, slot, :, 0:64], kvb[j][:, 128:256].rearrange("p (h d) -> p h d", h=2)); yield
                for i2 in range(0, len(kvt), 2):
                    merge(*[(kv_tile(j, p), 1) for p, j in enumerate(kvt[i2:i2 + 2])])
                rel(*kvbanks)

                if stage == 2: SC.dead = True
                Nc = min(512, S - (t0 + 16))
                cvb = []; cvx = []; cvxb = []
                for c in range(4):
                    wv, wr = WGET(('w_in', 0, 1280 + 128 * c))
                    b = bank(); cvb.append(b)
                    MM(b[:, 0:Nc], [(wv[:, kc, :], hT[:, kc, 16:16 + Nc]) for kc in range(8)])
                    if it == 0:
                        if c % 2 == 0:
                            bx = bank(); cvxb.append(bx)
                        bxs = bx[:, (c % 2) * 16:(c % 2) * 16 + 16]
                        cvx.append(bxs)
                        MM(bxs, [(wv[:, kc, :], hT[:, kc, 0:16]) for kc in range(8)])
                if it == 0:
                    MEMSET('pool', glu[:, :, 0:16], 0.0)
                else:
                    COPY('pool', glu[:, :, 0:32], gtail)
                if Nc < 512:
                    MEMSET('pool', glu[:, :, 32 + Nc:544], 0.0)
                for c in range(2):
                    ACT(sig[:, 0:Nc], cvb[2 + c][:, 0:Nc], AF.Sigmoid)
                    TT('dve', glu[:, c, 32:32 + Nc], cvb[c][:, 0:Nc], sig[:, 0:Nc], ALU.mult)
                    if it == 0:
                        ACT(stat[:, 16:32], cvx[2 + c], AF.Sigmoid)
                        TT('dve', glu[:, c, 16:32], cvx[c], stat[:, 16:32], ALU.mult)
                COPY('pool', gtail, glu[:, :, 512:544])
                rel(*cvb); rel(*cvxb)
                def taps_gen():
                    for k in range(31):
                        for c in range(2):
                            cw = PC_CW + 31 * c
                            if k == 0:
                                TS('dve', acc[:, c, :], glu[:, c, 1:513], pcols[:, cw:cw + 1], pcols[:, PC_CB + c:PC_CB + c + 1], ALU.mult, ALU.add)
                            else:
                                STT(acc[:, c, :], glu[:, c, 1 + k:513 + k], pcols[:, cw + k:cw + k + 1], acc[:, c, :], ALU.mult, ALU.add)
                            yield
                taps = taps_gen()
                def BG(n):
                    for _ in range(n):
                        try: next(taps)
                        except StopIteration: return
                def taps_n(n):
                    for _ in range(n):
                        try: next(taps)
                        except StopIteration: return
                        yield

                def conv_finish():
                    for _ in taps: yield
                    b1 = bank(); b2 = bank()
                    for c in range(2):
                        COPY('act', ybf[:, c, :], acc[:, c, :]); yield
                        MM(b1, [(ones, ybf[:, c, :])], first=(c == 0), last=(c == 1)); yield
                    for c in range(2):
                        ACT(sqb[:, c, 0:512], acc[:, c, :], AF.Square); yield
                        MM(b2, [(ones, sqb[:, c, 0:512])], first=(c == 0), last=(c == 1)); yield
                    rc = rstd_b[:, 0:512]
                    TS('dve', sig, b1, 1.0 / 256, None, ALU.mult); yield
                    TS('dve', rc, b2, 1.0 / 256, None, ALU.mult); rel(b1, b2); yield
                    msq = ybf_f.bitcast(F32)[:, 0:512]
                    TT('pool', msq, sig, sig, ALU.mult, r=[sig], w=[ybf_f]); yield
                    TT('dve', rc, rc, msq, ALU.subtract, r=[rc, ybf_f], w=[rc]); yield
                    ACT(rc, rc, AF.Ln, scale=1.0, bias=epsc[:, 0:1], r=[rc, epsc[:, 0:1]]); yield
                    ACT(rc, rc, AF.Exp, scale=-0.5); yield
                    for c in range(2):
                        TT('dve', acc[:, c, :], acc[:, c, :], sig, ALU.subtract); yield
                        TT('dve', acc[:, c, :], acc[:, c, :], rc, ALU.mult); yield
                        ACT(acc[:, c, :], acc[:, c, :], AF.Silu, scale=pcols[:, PC_CLG + c:PC_CLG + c + 1], bias=pcols[:, PC_CLB + c:PC_CLB + c + 1],
                            r=[acc[:, c, :], pcols[:, PC_CLG:PC_CLB + 2]]); yield
                    b3 = bank()
                    for c in range(2):
                        ACT(sqb[:, c, 0:512], acc[:, c, :], AF.Square); yield
                        MM(b3, [(ones, sqb[:, c, 0:512])], first=(c == 0), last=(c == 1)); yield
                    ACT(rc, b3, AF.Ln, scale=1.0 / 256, bias=epsc[:, 0:1], r=[b3, epsc[:, 0:1]]); rel(b3); yield
                    ACT(rc, rc, AF.Exp, scale=-0.5); yield
                    for c in range(2):
                        STT(mergedT[:, 6 + c, :], acc[:, c, :], pcols[:, PC_OGC + c:PC_OGC + c + 1], rc, ALU.mult, ALU.mult); yield

                if stage == 3: SC.dead = True
                gvb = {}; gvbanks = []
                for j in range(4):
                    if j % 2 == 0:
                        bcur = bank(); gvbanks.append(bcur)
                    gvb[j] = bcur[:, (j % 2) * 256:(j % 2) * 256 + 256]
                WHOLD()
                wgv = [WGET(('w_in', 4 * s2, 1024))[0] for s2 in range(2)]
                for j in range(4):
                    MM(gvb[j], [(hT[:, 4 * s2 + kk, j * 128:(j + 1) * 128], wgv[s2][:, kk, :]) for s2 in range(2) for kk in range(4)])
                WRELEASE()
                gub = []
                for c in range(2):
                    wv, wr = WGET(('w_in', 0, 768 + 128 * c))
                    b = bank(); gub.append(b)
                    MM(b, [(wv[:, kc, :], hT[:, kc, 0:512]) for kc in range(8)])
                for c in range(2):
                    ACT(uT[:, c, :], gub[c], AF.Gelu)
                rel(*gub)
                def gm_tile(j, par):
                    g_ = gvs[par]; vb_ = vn_bfs[par]; gt_ = gtmps[par]
                    ACT(g_[:, 0:256], gvb[j], AF.Gelu); yield
                    st6 = stat[:, 32 + 16 * par:38 + 16 * par]; mv = stat[:, 40 + 16 * par:42 + 16 * par]
                    SC.op('dve', lambda e: e.bn_stats(out=st6, in_=g_[:, 0:256]), r=[g_], w=[st6]); yield
                    SC.op('dve', lambda e: e.bn_aggr(out=mv, in_=st6), r=[st6], w=[mv]); yield
                    rs = stat[:, 44 + 16 * par:45 + 16 * par]
                    ACT(rs, mv[:, 1:2], AF.Ln, scale=1.0, bias=epsc[:, 0:1], r=[mv, epsc[:, 0:1]]); yield
                    ACT(rs, rs, AF.Exp, scale=-0.5); yield
                    TS('dve', g_[:, 0:256], g_[:, 0:256], mv[:, 0:1], rs, ALU.subtract, ALU.mult); yield
                    TT('dve', g_[:, 0:256], g_[:, 0:256], prow[:, PR_LNG:PR_LNG + 256], ALU.mult); yield
                    TT('dve', vb_[:, 0:256], g_[:, 0:256], prow[:, PR_LNB:PR_LNB + 256], ALU.add); yield
                    sgb = bank()
                    for c in range(2):
                        MM(sgb[:, c * 256:(c + 1) * 256], [(vb_[:, c * 128:(c + 1) * 128], wsT[:, 2 * c:2 * c + 2, :])]); yield
                    for c in range(2):
                        for hh in range(2):
                            ps_ = slice(64 * hh, 64 * hh + 64)
                            src = sgb[ps_, c * 256 + hh * 128:c * 256 + hh * 128 + 128]
                            TT('dve', gt_[ps_, c * 128:c * 128 + 128], src, bsT[ps_, c, :], ALU.add); yield
                            TT('pool', uT[ps_, c, j * 128:(j + 1) * 128], uT[ps_, c, j * 128:(j + 1) * 128], gt_[ps_, c * 128:c * 128 + 128], ALU.mult); yield
                    rel(sgb)
                merge((gm_tile(0, 0), 1), (gm_tile(1, 1), 1), (taps_n(12), 1))
                merge((gm_tile(2, 0), 1), (gm_tile(3, 1), 1), (taps_n(12), 1))
                rel(*gvbanks)
                b4 = bank()
                for c in range(2):
                    ACT(sqb[:, c, 0:512], uT[:, c, :], AF.Square)
                    MM(b4, [(ones, sqb[:, c, 0:512])], first=(c == 0), last=(c == 1))
                RSQ(rstd_b[:, 0:512], b4, 1.0 / 256); rel(b4)
                for c in range(2):
                    STT(mergedT[:, 4 + c, :], uT[:, c, :], pcols[:, PC_OGG + c:PC_OGG + c + 1], rstd_b[:, 0:512], ALU.mult, ALU.mult)

                if stage == 4: SC.dead = True
                WHOLD()
                qw = [WGET(('w_in', 2 * s4, 0))[0] for s4 in range(4)]
                def att_prep(j):
                    gt = 4 * it + j; par = j % 2
                    qbj = bank()
                    MM(qbj, [(hT[:, 2 * s4 + kk, j * 128:(j + 1) * 128], qw[s4][:, kk, :]) for s4 in range(4) for kk in range(2)]); yield
                    first = True
                    for _ in qk_prep(qbj, 8, PR_QG, q_fs[par], q_bfs[par], gt, sqtmps[par], stat[:, 8 + 8 * par:16 + 8 * par], sqtmps[par]):
                        if first: rel(qbj); first = False
                        yield
                    tb = bank(); tbb = tb[:].bitcast(BF16)
                    TRANS([tbb[0:64, h * 128:(h + 1) * 128] for h in range(8)], [q_bfs[par][:, h * 64:(h + 1) * 64] for h in range(8)],
                          r=[q_bfs[par]], w=[tbb[0:64, :]]); yield
                    COPY('dve', qT_sbs[par][0:64, :], tbb[0:64, :]); rel(tb); yield
                obs = {}
                def att_A(j):
                    gt = 4 * it + j; par = j % 2
                    qT_sb = qT_sbs[par]
                    blocks = [b for b in (gt - 1, gt, gt + 1) if 0 <= b < 32]
                    for kvh in range(2):
                        for b in blocks:
                            bi = kvh * 3 + (b - gt + 1)
                            sb2 = bank()
                            pairs = [(kT[0:64, kvh, b % 8, :], qT_sb[0:64, kvh * 512:(kvh + 1) * 512])]
                            if b != gt:
                                pairs.append((ident, negp if b < gt else negn))
                            MM(sb2, pairs); yield
                            ACT(PT[:, bi, :], sb2, AF.Exp, scale=0.125); rel(sb2); yield
                    ob = [bank(), bank()]; obs[j] = ob
                    for h in range(8):
                        kvh, hq = divmod(h, 4)
                        outp = ob[h // 4][:, (h % 4) * 65:(h % 4) * 65 + 65]
                        MM(outp, [(PT[:, kvh * 3 + (b - gt + 1), hq * 128:(hq + 1) * 128], vaug[:, b % 8, kvh, :]) for b in blocks]); yield
                def att_B(j):
                    par = j % 2
                    ob = obs[j]
                    den = stat[:, 64 + 8 * par:72 + 8 * par]
                    for hb in range(2):
                        o3 = ob[hb][:, 0:260].rearrange("p (h d) -> p h d", h=4)
                        TT('dve', den[:, hb * 4:hb * 4 + 4], o3[:, :, 64], esink[:, hb * 4:hb * 4 + 4], ALU.add, r=[ob[hb][:, 0:260], esink], w=[den]); yield
                    SC.op('dve', lambda e, den=den: e.reciprocal(out=den, in_=den), r=[den], w=[den]); yield
                    for hb in range(2):
                        o3 = ob[hb][:, 0:260].rearrange("p (h d) -> p h d", h=4)
                        TT('dve', o_f[:, hb * 256:(hb + 1) * 256].rearrange("p (h d) -> p h d", h=4), o3[:, :, 0:64],
                           den[:, hb * 4:hb * 4 + 4].unsqueeze(2).to_broadcast([128, 4, 64]), ALU.mult,
                           r=[ob[hb][:, 0:260], den], w=[o_f[:, hb * 256:(hb + 1) * 256]]); yield
                    rel(*ob)
                    ss = stat[:, 80 + par:81 + par]
                    ACT(m_bf, o_f, AF.Square, accum_out=ss, w=[m_bf, ss]); yield
                    ACT(ss, ss, AF.Ln, scale=1.0 / 512, bias=epsc[:, 0:1], r=[ss, epsc[:, 0:1]]); yield
                    ACT(ss, ss, AF.Exp, scale=-0.5); yield
                    TS('dve', m_bf, o_f, ss, None, ALU.mult); yield
                    tb2 = bank(); tbb2 = tb2[:].bitcast(BF16)
                    TRANS([tbb2[:, c * 128:(c + 1) * 128] for c in range(4)], [m_bf[:, c * 128:(c + 1) * 128] for c in range(4)],
                          r=[m_bf], w=[tbb2[:, 0:512]]); yield
                    for c in range(4):
                        ACT(mergedT[:, c, j * 128:(j + 1) * 128], tbb2[:, c * 128:(c + 1) * 128], AF.Identity, scale=pcols[:, PC_OGA + c:PC_OGA + c + 1],
                            r=[tbb2[:, c * 128:(c + 1) * 128], pcols[:, PC_OGA:PC_OGA + 4]]); yield
                    rel(tb2)
                merge((att_prep(0), 1), (taps_n(8), 1))
                merge((att_prep(1), 1), (att_A(0), 1), (taps_n(15), 1))
                for j in range(4):
                    gens = [(att_B(j), 1)]
                    if j + 1 < 4: gens.append((att_A(j + 1), 2))
                    if j + 2 < 4: gens.append((att_prep(j + 2), 1))
                    if j == 0: gens.append((taps_n(15), 1))
                    if j == 1: gens.append((conv_finish(), 2))
                    merge(*gens)

                if stage == 5: SC.dead = True
                WRELEASE()
                for m in range(8):
                    wv, wr = WGET(('w_out', 0, 128 * m))
                    b = bank()
                    MM(b, [(wv[:, kc, :], mergedT[:, kc, :]) for kc in range(8)])
                    TT('dve', X[:, m, t0:t0 + T], b, X[:, m, t0:t0 + T], ALU.add); rel(b)
                    if m == 0:
                        pcsF = norm_pieces(T); bksF = [bank() for _ in pcsF]
                    else:
                        norm_stats(m - 1, T, t0, bksF, pcsF)
                norm_stats(7, T, t0, bksF, pcsF)
                norm_fin(bksF, pcsF, rstd_b)
                ACT(stat[:, 100:101], stat[:, 100:101], AF.Silu)
                norm_apply(PC_FFN, T, t0, rstd_b)

                if stage == 6: SC.dead = True
                for (f0, f1) in ((0, 16), (16, 22)):
                    for f in range(f0, f1):
                        wg, _ = WGET(('w_gate_up', 0, 128 * f))
                        bg = bank()
                        MM(bg, [(wg[:, kc, :], hT[:, kc, 0:T]) for kc in range(8)], split=(f == 0))
                        wu, _ = WGET(('w_gate_up', 0, DFF + 128 * f))
                        bu = bank()
                        MM(bu, [(wu[:, kc, :], hT[:, kc, 0:T]) for kc in range(8)])
                        sgt = sg2[f % 2]
                        ACT(sgt, bg, AF.Silu); rel(bg)
                        TT('dve', actb[:, f - f0, :], bu, sgt, ALU.mult); rel(bu)
                        if nxt_t0 is not None and f < 8:
                            Wn_ = min(640, S - nxt_t0)
                            if f == 0:
                                pcsN = norm_pieces(Wn_); bksN = [bank() for _ in pcsN]
                            norm_stats(f, Wn_, nxt_t0, bksN, pcsN)
                            if f == 7:
                                norm_fin(bksN, pcsN, rstd_n); prestat[0] = True
                    for m in range(8):
                        b = bank()
                        pieces = list(range(f0, f1, 8))
                        for pi, k0 in enumerate(pieces):
                            nk = min(8, f1 - k0)
                            wv, _ = WGET(('w_down', k0, 128 * m))
                            MM(b, [(wv[:, kk, :], actb[:, k0 - f0 + kk, :]) for kk in range(nk)],
                               first=(pi == 0), last=(pi == len(pieces) - 1))
                        TT('dve', X[:, m, t0:t0 + T], b, X[:, m, t0:t0 + T], ALU.add); rel(b)
                        if f0 == 16:
                            if m == 0:
                                pcsP = norm_pieces(T); bksP = [bank() for _ in pcsP]
                            else:
                                norm_stats(m - 1, T, t0, bksP, pcsP)

                if stage == 7: SC.dead = True
                norm_stats(7, T, t0, bksP, pcsP)
                norm_fin(bksP, pcsP, rstd_b)
                ACT(stat[:, 101:102], stat[:, 101:102], AF.Sigmoid)
                norm_apply(PC_PLE, T, t0, rstd_b)
                pv = pTd[l].rearrange("(c p) t -> p c t", p=128)
                SC.dma('pool', 'pl', lambda e, s, pv=pv, t0=t0: e.dma_start(out=pTb, in_=pv[:, :, t0:t0 + T]).then_inc(s, 16), w=[pTb_f])
                for m in range(8):
                    if m % 4 == 0:
                        WHOLD()
                        pjw, _ = WGET(('w_ple_proj', 0, 512 * (m // 4)))
                    wg, _ = WGET(('w_ple_gate', 0, 128 * m))
                    bg = bank()
                    MM(bg, [(wg[:, kc, :], hT[:, kc, 0:T]) for kc in range(8)], split=(m == 0))
                    bp = bank()
                    MM(bp, [(pjw[:, kk, (m % 4) * 128:(m % 4) * 128 + 128], pTb[:, kk, :]) for kk in range(2)])
                    if m % 4 == 3: WRELEASE()
                    sgt = sg2[m % 2]
                    ACT(sgt, bg, AF.Sigmoid); rel(bg)
                    TT('dve', ptmp[m % 2], bp, sgt, ALU.mult); rel(bp)
                    TT('pool', X[:, m, t0:t0 + T], X[:, m, t0:t0 + T], ptmp[m % 2], ALU.add)
                SC.dead = False; SC.maxrec = 10**9
                ACT(stat[:, 102:103], stat[:, 101:102], AF.Exp)
                if li == len(layers) - 1:
                    SC.dma('pool', 'out', lambda e, s, t0=t0: e.dma_start(out=oTv[:, :, t0:t0 + T], in_=X[:, :, t0:t0 + T]).then_inc(s, 16),
                           r=[X[:, kc, t0:t0 + T] for kc in range(8)])
        SC.wait('sp', ('out', SC.cnt['out']))
        SC.wait('pool', ('out', SC.cnt['out']))
        for s in [f'wst{i}' for i in range(4)]:
            if SC.cnt[s]: SC.wait('sp', (s, SC.cnt[s]))
        SC.replay(block)
    return nc


def _host_layout(inp):
    f32 = np.float32
    g = {k: np.asarray(v) for k, v in inp.items()}
    L = DEPTH
    pcols = np.zeros((L, 128, NPC), f32)
    prow = np.zeros((L, 1, NPR), f32)
    for l in range(L):
        def col(v, n): return np.ascontiguousarray(v.reshape(n, 128).T)
        pcols[l, :, PC_MIX:PC_MIX + 8] = col(g['norm_mix_g'][l], 8)
        pcols[l, :, PC_FFN:PC_FFN + 8] = col(g['norm_ffn_g'][l], 8)
        pcols[l, :, PC_PLE:PC_PLE + 8] = col(g['ple_norm_g'][l], 8)
        og = g['out_norm_g'][l]
        pcols[l, :, PC_OGA:PC_OGA + 4] = col(og[0:512], 4)
        pcols[l, :, PC_OGG:PC_OGG + 2] = col(og[512:768], 2)
        pcols[l, :, PC_OGC:PC_OGC + 2] = col(og[768:1024], 2)
        cw = g['conv_w'][l]
        for c in range(2):
            pcols[l, :, PC_CW + 31 * c:PC_CW + 31 * c + 31] = cw[:, c * 128:(c + 1) * 128].T
        pcols[l, :, PC_CB:PC_CB + 2] = col(g['conv_b'][l], 2)
        pcols[l, :, PC_CLG:PC_CLG + 2] = col(g['conv_ln_g'][l], 2)
        pcols[l, :, PC_CLB:PC_CLB + 2] = col(g['conv_ln_b'][l], 2)
        prow[l, 0, PR_QG:PR_QG + 64] = g['q_norm_g'][l]
        prow[l, 0, PR_KG:PR_KG + 64] = g['k_norm_g'][l]
        prow[l, 0, PR_SINK:PR_SINK + 8] = g['sink'][l]
        prow[l, 0, PR_LNG:PR_LNG + 256] = g['gm_ln_g'][l]
        prow[l, 0, PR_LNB:PR_LNB + 256] = g['gm_ln_b'][l]
    wsT = np.ascontiguousarray(g['gm_ws'].transpose(0, 3, 1, 2)).reshape(L, 128, 512).astype(f32)
    consts = np.zeros((128, 392), f32)
    consts[:, 0:128] = np.eye(128, dtype=f32)
    jj = np.arange(128)[:, None]; ii = np.arange(128)[None, :]
    consts[:, 128:256] = (jj >= ii).astype(f32)
    consts[:, 256:384] = (jj <= ii).astype(f32)
    consts[:, 384:392] = (500000.0 ** (-np.arange(0, 16, 2, dtype=f32) / 16)).astype(f32)[None, :]
    common = dict(consts=consts, pcols=pcols, prow=prow, gm_bs=np.ascontiguousarray(g['gm_bs'], f32), wsT=wsT)
    for k in ('w_in', 'w_out', 'w_gate_up', 'w_down', 'w_ple_gate', 'w_ple_proj'):
        common[k] = np.ascontiguousarray(g[k], f32)
    maps = []
    for c in range(NCORES):
        m = dict(common)
        m['xT'] = np.ascontiguousarray(g['x'][c].T)
        m['pT'] = np.ascontiguousarray(g['p'][:, c].transpose(0, 2, 1))
        m['pos'] = np.ascontiguousarray(g['positions'][c].reshape(32, 128).T).astype(np.int32)
        maps.append(m)
    return maps


_NC_CACHE = {}


def kernel(**inputs):
    maps = _host_layout(inputs)
    key = 'fused'
    if key not in _NC_CACHE:
        _NC_CACHE[key] = build_nc([0, 1], True)
    nc = _NC_CACHE[key]
    res = run_bass_kernel_spmd(nc, maps, core_ids=list(range(NCORES)))
    out = np.stack([np.ascontiguousarray(res.results[c]["oT"].T) for c in range(NCORES)], axis=0)
    return out.astype(np.float32)
```

```python
import math
from contextlib import ExitStack
import numpy as np
import concourse.bass as bass
import concourse.mybir as mybir
from concourse.bass_utils import run_bass_kernel_spmd

F32 = mybir.dt.float32; BF16 = mybir.dt.bfloat16; I32 = mybir.dt.int32
AF = mybir.ActivationFunctionType; ALU = mybir.AluOpType; AX = mybir.AxisListType

NCORES = 8
D = 1024; S = 4096; DEPTH = 2; T = 512; NG = S // T
DFF = 2816; INC = 1792; PLE = 256
EPS = 1e-6
NSLOT = 8
CELL = 64


def _box(ap):
    es = mybir.dt.size(ap.dtype)
    a = ap.ap
    ps = a[0][0]
    off = ap.offset
    fo = off % ps if ps > 0 else off
    ext = 0
    for (st, cn) in a[1:]:
        ext += abs(st) * (cn - 1)
    lo = fo * es
    hi = (fo + ext + 1) * es
    if ap.tensor.name.startswith('bk'):
        return (ap.tensor.name, 0, 0)
    return (ap.tensor.name, lo // CELL, (hi - 1) // CELL)


class Sched:
    ENG = ['pe', 'act', 'dve', 'pool', 'sp']

    def __init__(self, nc, sems):
        self.nc = nc
        self.sem = sems
        self.cnt = {k: 0 for k in sems}
        self.prog = {e: [] for e in self.ENG}
        self.seen = {e: {} for e in self.ENG}
        self.cells = {}
        self.last = {}
        self.dead = False
        self.nrec = 0
        self.maxrec = 10**9
        self.log = []

    def _deps(self, reads, writes):
        deps = {}
        for ap in reads:
            n, lo, hi = _box(ap)
            for c in range(lo, hi + 1):
                st = self.cells.get((n, c))
                if st and st[0]:
                    s, v = st[0]
                    if deps.get(s, 0) < v: deps[s] = v
        for ap in writes:
            n, lo, hi = _box(ap)
            for c in range(lo, hi + 1):
                st = self.cells.get((n, c))
                if st:
                    if st[0]:
                        s, v = st[0]
                        if deps.get(s, 0) < v: deps[s] = v
                    for s, v in st[1].items():
                        if deps.get(s, 0) < v: deps[s] = v
        return deps

    def _emit_waits(self, eng, deps):
        for s, v in deps.items():
            if s == 'pe' and eng == 'pe': continue
            if self.seen[eng].get(s, 0) >= v: continue
            self.seen[eng][s] = v
            self.prog[eng].append(('wait', s, v))

    def _mark(self, reads, writes, tick):
        s, v = tick
        for ap in reads:
            n, lo, hi = _box(ap)
            for c in range(lo, hi + 1):
                st = self.cells.setdefault((n, c), [None, {}])
                if st[1].get(s, 0) < v: st[1][s] = v
        for ap in writes:
            n, lo, hi = _box(ap)
            for c in range(lo, hi + 1):
                self.cells[(n, c)] = [tick, {}]

    def op(self, eng, fn, r=(), w=(), extra=()):
        if self.dead: return ('pe', 0)
        self.nrec += 1
        if self.nrec > self.maxrec: return ('pe', 0)
        import traceback
        self.log.append((self.nrec, eng, traceback.extract_stack()[-3].lineno, traceback.extract_stack()[-2].lineno))
        deps = self._deps(r, w)
        for sv in extra:
            if sv is not None and deps.get(sv[0], 0) < sv[1]: deps[sv[0]] = sv[1]
        self._emit_waits(eng, deps)
        self.cnt[eng] += 1
        tick = (eng, self.cnt[eng])
        self.prog[eng].append(('ins', fn, eng))
        self._mark(r, w, tick)
        return tick

    def dma(self, q, semname, fn, r=(), w=(), extra=()):
        if self.dead: return ('pe', 0)
        deps = self._deps(r, w)
        for sv in list(extra) + [self.last.get(semname)]:
            if sv is not None and deps.get(sv[0], 0) < sv[1]: deps[sv[0]] = sv[1]
        self._emit_waits(q, deps)
        self.cnt[semname] += 16
        tick = (semname, self.cnt[semname])
        self.last[semname] = tick
        self.prog[q].append(('dma', fn, semname))
        self._mark(r, w, tick)
        return tick

    def wait(self, eng, tick):
        self._emit_waits(eng, {tick[0]: tick[1]})

    def replay(self, block):
        engs = {'pe': block.tensor, 'act': block.scalar, 'dve': block.vector,
                'pool': block.gpsimd, 'sp': block.sync}
        for en, deco in engs.items():
            prog = self.prog[en]

            def body(e, prog=prog):
                for it in prog:
                    if it[0] == 'wait':
                        e.wait_ge(self.sem[it[1]], it[2])
                    elif it[0] == 'ins':
                        it[1](e).then_inc(self.sem[it[2]], 1)
                    else:
                        it[1](e, self.sem[it[2]])
            deco(body)


def slab_table():
    t = []
    for s in range(2): t.append(('w_in', 4 * s, 4, 512, 256))
    for c in range(4): t.append(('w_in', 0, 8, 1280 + 128 * c, 128))
    for s in range(2): t.append(('w_in', 4 * s, 4, 1024, 256))
    for c in range(2): t.append(('w_in', 0, 8, 768 + 128 * c, 128))
    for s in range(4): t.append(('w_in', 2 * s, 2, 0, 512))
    for m in range(8): t.append(('w_out', 0, 8, 128 * m, 128))
    for (f0, f1) in ((0, 16), (16, 22)):
        for f in range(f0, f1):
            t.append(('w_gate_up', 0, 8, 128 * f, 128))
            t.append(('w_gate_up', 0, 8, DFF + 128 * f, 128))
        for m in range(8):
            for k0 in range(f0, f1, 8):
                t.append(('w_down', k0, min(8, f1 - k0), 128 * m, 128))
    for s in range(2):
        t.append(('w_ple_proj', 0, 2, 512 * s, 512))
        for m in range(4 * s, 4 * s + 4): t.append(('w_ple_gate', 0, 8, 128 * m, 128))
    return t


NPC = 104
PC_MIX, PC_FFN, PC_PLE, PC_OGA, PC_OGG, PC_OGC, PC_CW, PC_CB, PC_CLG, PC_CLB = 0, 8, 16, 24, 28, 30, 32, 94, 96, 98
NPR = 648
PR_QG, PR_KG, PR_SINK, PR_LNG, PR_LNB = 0, 64, 128, 136, 392


def build_nc(layers, last, ngrun=NG, stage=99):
    import os as _os2
    nc = bass.Bass("TRN2", target_bir_lowering=False, dynamic_dma_scratch_size=int(_os2.environ.get("KSCR", 4096)))
    dram = {}
    def din(name, shape, dt=F32):
        dram[name] = nc.dram_tensor(name, shape, dt, kind="ExternalInput").ap()
        return dram[name]
    xT = din("xT", [D, S]); pTd = din("pT", [DEPTH, PLE, S]); posd = din("pos", [128, 32], I32)
    constd = din("consts", [128, 392]); pcolsd = din("pcols", [DEPTH, 128, NPC]); prowd = din("prow", [DEPTH, 1, NPR])
    bsd = din("gm_bs", [DEPTH, 4, 128]); wsd = din("wsT", [DEPTH, 128, 512])
    wd = {}
    for name, shp in (("w_in", [DEPTH, D, INC]), ("w_out", [DEPTH, D, D]), ("w_gate_up", [DEPTH, D, 2 * DFF]),
                      ("w_down", [DEPTH, DFF, D]), ("w_ple_gate", [DEPTH, D, D]), ("w_ple_proj", [DEPTH, PLE, D])):
        wd[name] = din(name, shp)
    oT = nc.dram_tensor("oT", [D, S], F32, kind="ExternalOutput").ap()
    slabs = slab_table(); NS = len(slabs)
    scr = nc.dram_tensor("wscr", [DEPTH * NS, 128, 1024], BF16, kind="Internal").ap()

    with ExitStack() as es:
        X = es.enter_context(nc.sbuf_tensor("X", [128, 8, S], F32))
        ARENA_F = (81664 + 16384 - int(_os2.environ.get('KSCR', 4096))) // 4
        A = es.enter_context(nc.sbuf_tensor("A", [128, ARENA_F], F32))
        banks = [es.enter_context(nc.psum_tensor(f"bk{i}", [128, 512], F32)) for i in range(8)]
        semnames = ['pe', 'act', 'dve', 'pool', 'sp', 'xl0', 'xl1', 'par', 'pl', 'st', 'out'] + \
                   [f'ws{i}' for i in range(NSLOT)] + [f'wq{i}' for i in range(NSLOT)] + [f'wst{i}' for i in range(4)] + [f'p{i}' for i in range(10)]
        sems = {n: es.enter_context(nc.semaphore(n)) for n in semnames}
        block = es.enter_context(nc.Block())
        SC = Sched(nc, sems)
        import os
        SC.maxrec = int(os.environ.get('KMAX', 10**9))
        build_nc.SC = SC

        cur = [0]
        def carve(nbytes, dt=F32, shape=None, at=None):
            if at is None:
                off = cur[0]; cur[0] += (nbytes + 63) // 64 * 64
            else:
                off = at
            assert off % 4 == 0 and off + nbytes <= ARENA_F * 4, (off, nbytes)
            v = A[:, off // 4:(off + nbytes + 3) // 4]
            if dt != F32: v = v.bitcast(dt)
            return v, off
        def view3(v, pat, **kw): return v.rearrange(pat, **kw)

        hT_f, _ = carve(8 * 640 * 2, BF16); hT = hT_f.rearrange("p (k t) -> p k t", k=8)
        ring = [carve(2048, BF16)[0] for _ in range(NSLOT)]
        mT_f, _ = carve(8 * 512 * 2, BF16); mergedT = mT_f.rearrange("p (k t) -> p k t", k=8)
        rstd_b, _ = carve(640 * 4)
        sq_f, _ = carve(2 * 640 * 2, BF16); sqb = sq_f.rearrange("p (k t) -> p k t", k=2)
        ident, _ = carve(256, BF16); ones, _ = carve(256, BF16)
        negp, _ = carve(1024, BF16); negn, _ = carve(1024, BF16)
        cos_f, _ = carve(1024); sin_f, _ = carve(1024)
        cosT = cos_f.rearrange("p (n j) -> p n j", j=8); sinT = sin_f.rearrange("p (n j) -> p n j", j=8)
        mhalf, _ = carve(64)
        pcols, _ = carve(NPC * 4); prow, _ = carve(NPR * 4)
        bsT_f, _ = carve(1024); bsT = bsT_f.rearrange("p (c q) -> p c q", c=2)
        wsT_f, _ = carve(1024, BF16); wsT = wsT_f.rearrange("p (h q) -> p h q", h=4)
        esink, _ = carve(64)
        kT_f, _ = carve(4096, BF16); kT = kT_f.rearrange("p (h s t) -> p h s t", h=2, s=8)
        va_f, _ = carve(8 * 2 * 65 * 2 + 32, BF16); vaug = va_f[:, 0:8 * 2 * 65].rearrange("p (s h d) -> p s h d", s=8, h=2)
        gtail_f, _ = carve(256); gtail = gtail_f.rearrange("p (c j) -> p c j", c=2)
        stat, _ = carve(512)
        U0 = cur[0]
        USZ = ARENA_F * 4 - U0
        assert USZ >= 36096, USZ
        def ucarve(off, nbytes, dt=F32): return carve(nbytes, dt, at=U0 + off)[0]
        rstd_n = ucarve(22528, 640 * 4)
        pTb_f = ucarve(20480, 2048, BF16); pTb = pTb_f.rearrange("p (c t) -> p c t", c=2)
        act_f = ucarve(0, 16 * 512 * 2, BF16); actb = act_f.rearrange("p (k t) -> p k t", k=16)
        sg2 = [ucarve(16384, 2048), ucarve(18432, 2048)]
        ptmp = [ucarve(0, 2048), ucarve(2048, 2048)]
        q_fs = [ucarve(0, 2048), ucarve(2048, 2048)]
        sqtmps = [ucarve(4096, 2048), ucarve(18432, 2048)]
        q_bfs = [ucarve(6144, 1024, BF16), ucarve(7168, 1024, BF16)]
        qT_sbs = [ucarve(8192, 2048, BF16), ucarve(10240, 2048, BF16)]
        PT_f = ucarve(12288, 6144, BF16); PT = PT_f.rearrange("p (i t) -> p i t", i=6)
        sqtmps[1] = ucarve(20480, 2048)
        o_f = ucarve(18432, 2048)
        k_fs = [ucarve(12288, 512), ucarve(12800, 512)]; k_bfs = [ucarve(13312, 256, BF16), ucarve(13568, 256, BF16)]
        k_sq = [ucarve(13824, 512), ucarve(14336, 512)]; rtmps = [ucarve(14848, 256), ucarve(15104, 256)]
        m_bf = ucarve(35072, 1024, BF16)
        glu_f = ucarve(22528, 4352); glu = glu_f.rearrange("p (c j) -> p c j", c=2)
        sig = ucarve(26880, 2048); acc_f = ucarve(28928, 4096); acc = acc_f.rearrange("p (c t) -> p c t", c=2)
        ybf_f = ucarve(33024, 2048, BF16); ybf = ybf_f.rearrange("p (c t) -> p c t", c=2)
        gvs = [ucarve(0, 1024), ucarve(1024, 1024)]; vn_bfs = [ucarve(2048, 512, BF16), ucarve(2560, 512, BF16)]
        uT_f = ucarve(3072, 4096); uT = uT_f.rearrange("p (c t) -> p c t", c=2)
        gtmps = [ucarve(7168, 1024), ucarve(8192, 1024)]

        freeb = list(range(8))
        def bank():
            assert freeb, "no free PSUM bank"
            return banks[freeb.pop(0)][:]
        def rel(*aps):
            for ap in aps:
                i = int(ap.tensor.name[2:])
                assert i not in freeb
                freeb.append(i)

        def ACT(out, in_, func, r=None, w=None, **kw):
            return SC.op('act', lambda e: e.activation(out=out, in_=in_, func=func, **kw),
                         r=(r if r is not None else [in_]), w=(w if w is not None else [out]))
        def TT(eng, out, in0, in1, op, r=None, w=None):
            return SC.op(eng, lambda e: e.tensor_tensor(out=out, in0=in0, in1=in1, op=op),
                         r=(r if r is not None else [in0, in1]), w=(w if w is not None else [out]))
        def TS(eng, out, in0, s1, s2, op0, op1=None, r=None, w=None):
            rr = [in0] + [s for s in (s1, s2) if not isinstance(s, (int, float, type(None)))]
            if op1 is None:
                return SC.op(eng, lambda e: e.tensor_scalar(out=out, in0=in0, scalar1=s1, scalar2=None, op0=op0),
                             r=(r if r is not None else rr), w=(w if w is not None else [out]))
            return SC.op(eng, lambda e: e.tensor_scalar(out=out, in0=in0, scalar1=s1, scalar2=s2, op0=op0, op1=op1),
                         r=(r if r is not None else rr), w=(w if w is not None else [out]))
        def STT(out, in0, scalar, in1, op0, op1, r=None, w=None):
            rr = [in0, in1] + ([] if isinstance(scalar, (int, float)) else [scalar])
            return SC.op('dve', lambda e: e.scalar_tensor_tensor(out=out, in0=in0, scalar=scalar, in1=in1, op0=op0, op1=op1),
                         r=(r if r is not None else rr), w=(w if w is not None else [out]))
        def COPY(eng, out, in_):
            if eng == 'act':
                return ACT(out, in_, AF.Copy)
            return SC.op(eng, lambda e: e.tensor_copy(out=out, in_=in_), r=[in_], w=[out])
        def MEMSET(eng, ap, val):
            return SC.op(eng, lambda e: e.memset(ap, val), w=[ap])
        def MM(out, pairs, r=None, first=True, last=True, split=False):
            if split and len(pairs) > 1:
                for i, pr in enumerate(pairs):
                    MM(out, [pr], first=(first and i == 0), last=(last and i == len(pairs) - 1))
                return
            def fn(e):
                n = len(pairs)
                for i, (l, rh) in enumerate(pairs):
                    ins = e.matmul(out, lhsT=l, rhs=rh, start=(first and i == 0), stop=(last and i == n - 1))
                return ins
            rr = r if r is not None else [a for pr in pairs for a in pr]
            return SC.op('pe', fn, r=rr, w=[out])
        def TRANS(out_list, in_list, r, w):
            def fn(e):
                for o, i in zip(out_list, in_list):
                    ins = e.transpose(out=o, in_=i, identity=ident)
                return ins
            return SC.op('pe', fn, r=list(r) + [ident], w=w)
        def POW(ap):
            shp = list(ap.shape)
            return TT('pool', ap, ap, mhalf[:, 0:1].to_broadcast(shp) if len(shp) == 2 else mhalf[:, 0:1].unsqueeze(2).to_broadcast(shp), ALU.pow,
                      r=[ap, mhalf[:, 0:1]], w=[ap])

        epsc = mhalf
        def RSQ(out, in_, scale):
            ACT(out, in_, AF.Ln, scale=scale, bias=epsc[:, 0:1], r=[in_, epsc[:, 0:1]])
            ACT(out, out, AF.Exp, scale=-0.5)

        wstate = {'next': 0, 'store': {}}
        def w_src(l, i):
            name, kc0, nk, c0, ncols = slabs[i]
            Wv = wd[name][l].rearrange("(kc p) c -> p kc c", p=128)
            return Wv[:, kc0:kc0 + nk, c0:c0 + ncols], nk, ncols
        def w_issue(gidx):
            li, rem = divmod(gidx, NG * NS)
            if li >= len(layers): return
            l = layers[li]
            it, i = divmod(rem, NS)
            slot = gidx % NSLOT
            src, nk, ncols = w_src(l, i)
            dst = ring[slot][:, 0:nk * ncols].rearrange("p (k c) -> p k c", k=nk)
            if it == 0:
                SC.dma('pool', f'wq{slot}', lambda e, s: e.dma_start(out=dst, in_=src).then_inc(s, 16), w=[ring[slot]])
                sidx = l * NS + i
                stsem = f'wst{gidx % 4}'
                tk = SC.dma('sp', stsem, lambda e, s: e.dma_start(out=scr[sidx], in_=ring[slot]).then_inc(s, 16), r=[ring[slot]])
                wstate['store'][(l, i)] = tk
            else:
                sidx = l * NS + i
                SC.dma('sp', f'ws{slot}', lambda e, s: e.dma_start(out=ring[slot], in_=scr[sidx]).then_inc(s, 16),
                       w=[ring[slot]], extra=[wstate['store'].get((l, i))])
        wctr = [0]
        held = [None]
        def WHOLD(): held[0] = wctr[0]
        def WRELEASE(): held[0] = None
        def WGET(expect):
            g = wctr[0]; wctr[0] += 1
            base = g if held[0] is None else held[0]
            while wstate['next'] <= base + NSLOT - 1:
                w_issue(wstate['next']); wstate['next'] += 1
            i = g % NS
            name, kc0, nk, c0, ncols = slabs[i]
            assert (name, kc0, c0) == expect, (slabs[i], expect)
            slot = g % NSLOT
            return ring[slot][:, 0:nk * ncols].rearrange("p (k c) -> p k c", k=nk), ring[slot]

        xTv = xT.rearrange("(kc p) t -> p kc t", p=128)
        oTv = oT.rearrange("(kc p) t -> p kc t", p=128)
        def load_x(g):
            SC.dma('sp', f'xl{g % 2}', lambda e, s: e.dma_start(out=X[:, :, g * T:(g + 1) * T], in_=xTv[:, :, g * T:(g + 1) * T]).then_inc(s, 16),
                   w=[X[:, kc, g * T:(g + 1) * T] for kc in range(8)])
        load_x(0)
        if ngrun > 1: load_x(1)
        cst = ucarve(8192, 392 * 4)
        SC.dma('sp', 'p0', lambda e, s: e.dma_start(out=cst, in_=constd[:, :]).then_inc(s, 16), w=[cst])
        posi = ucarve(9792, 128, I32); posf = ucarve(9920, 128)
        SC.dma('sp', 'p1', lambda e, s: e.dma_start(out=posi, in_=posd[:, :]).then_inc(s, 16), w=[posi])
        COPY('dve', ident, cst[:, 0:128])
        for g4 in range(4):
            TS('dve', negp[:, g4 * 128:(g4 + 1) * 128], cst[:, 128:256], -1.0, 30000.0, ALU.add, ALU.mult)
            TS('dve', negn[:, g4 * 128:(g4 + 1) * 128], cst[:, 256:384], -1.0, 30000.0, ALU.add, ALU.mult)
        MEMSET('dve', ones, 1.0); MEMSET('dve', epsc, EPS)
        MEMSET('dve', va_f, 1.0)
        MEMSET('dve', stat, 0.0)
        COPY('dve', posf, posi)
        def setup_rope():
            angt = ucarve(0, 1024); ang3 = angt.rearrange("p (n j) -> p n j", j=8)
            ut = ucarve(1024, 1024); ki = ucarve(2048, 1024, I32); kf = ucarve(3072, 1024)
            TT('dve', ang3, posf.unsqueeze(2).to_broadcast([128, 32, 8]), cst[:, 384:392].unsqueeze(1).to_broadcast([128, 32, 8]), ALU.mult,
               r=[posf, cst[:, 384:392]], w=[angt])
            for (dstT, shift) in ((sin_f, 0.0), (cos_f, 0.25)):
                TS('dve', ut, angt, 1.0 / (2 * math.pi), shift, ALU.mult, ALU.add)
                COPY('dve', ki, ut); COPY('dve', kf, ki)
                TT('dve', ut, ut, kf, ALU.subtract)
                TS('dve', ut, ut, 2 * math.pi, 3.14159, ALU.mult, ALU.min)
                TS('dve', ut, ut, -3.14159, None, ALU.max)
                ACT(dstT, ut, AF.Sin)


        for li, l in enumerate(layers):
            SC.dma('sp', 'p2', lambda e, s, l=l: e.dma_start(out=pcols, in_=pcolsd[l]).then_inc(s, 16), w=[pcols])
            SC.dma('sp', 'p3', lambda e, s, l=l: e.dma_start(out=prow, in_=prowd[l, 0, :].partition_broadcast(128)).then_inc(s, 16), w=[prow])
            for h in range(4):
                c, j = divmod(h, 2)
                SC.dma('sp', f'p{4 + h}', lambda e, s, l=l, h=h, c=c, j=j: e.dma_start(out=bsT[64 * j:64 * j + 64, c, :], in_=bsd[l, h, :].partition_broadcast(64)).then_inc(s, 16),
                       w=[bsT[:, c, :]])
            wstg = ucarve(0, 2048)
            SC.dma('sp', 'p8', lambda e, s, l=l: e.dma_start(out=wstg, in_=wsd[l]).then_inc(s, 16), w=[wstg])
            COPY('dve', wsT_f, wstg)
            ACT(esink[:, 0:8], prow[:, PR_SINK:PR_SINK + 8], AF.Exp)

            for it in range(ngrun):
                if li == 0 and it == 0: prestat = [False]
                t0 = it * T
                if stage == 0: SC.dead = True
                if li == 0 and it + 2 < ngrun:
                    load_x(it + 2)
                W = min(640, S - t0)

                def norm_pieces(Wn): return [(0, min(512, Wn))] + ([(512, Wn)] if Wn > 512 else [])
                def norm_stats(kc, Wn, tcol, bks, pieces):
                    ACT(sqb[:, kc % 2, 0:Wn], X[:, kc, tcol:tcol + Wn], AF.Square)
                    for (c0, c1), bk in zip(pieces, bks):
                        MM(bk[:, 0:c1 - c0], [(ones, sqb[:, kc % 2, c0:c1])], first=(kc == 0), last=(kc == 7))
                def norm_fin(bks, pieces, rbuf):
                    for (c0, c1), bk in zip(pieces, bks):
                        RSQ(rbuf[:, c0:c1], bk[:, 0:c1 - c0], 1.0 / D)
                    rel(*bks)
                def norm_apply(pc0, Wn, tcol, rbuf):
                    for kc in range(8):
                        STT(hT[:, kc, 0:Wn], X[:, kc, tcol:tcol + Wn], pcols[:, pc0 + kc:pc0 + kc + 1], rbuf[:, 0:Wn], ALU.mult, ALU.mult)

                if not prestat[0]:
                    pcs = norm_pieces(W); bks = [bank() for _ in pcs]
                    for kc in range(8): norm_stats(kc, W, t0, bks, pcs)
                    norm_fin(bks, pcs, rstd_n)
                norm_apply(PC_MIX, W, t0, rstd_n)
                prestat[0] = False
                if li == 0 and it == 0: setup_rope()
                if it + 1 < ngrun: nxt_t0 = (it + 1) * T
                elif li + 1 < len(layers): nxt_t0 = 0
                else: nxt_t0 = None

                if stage == 1: SC.dead = True
                kvt = [j for j in range(0 if it == 0 else 1, 5) if t0 + j * 128 < S]
                kvb = {}; kvbanks = []
                for idx, j in enumerate(kvt):
                    if idx % 2 == 0:
                        bcur = bank(); kvbanks.append(bcur)
                    kvb[j] = bcur[:, (idx % 2) * 256:(idx % 2) * 256 + 256]
                WHOLD()
                wkv = [WGET(('w_in', 4 * s2, 512))[0] for s2 in range(2)]
                for j in kvt:
                    MM(kvb[j], [(hT[:, 4 * s2 + kk, j * 128:(j + 1) * 128], wkv[s2][:, kk, :]) for s2 in range(2) for kk in range(4)], split=(j == kvt[0]))
                WRELEASE()

                def merge(*gens):
                    active = [[g, n] for g, n in gens]
                    while active:
                        for ent in list(active):
                            for _ in range(ent[1]):
                                try:
                                    next(ent[0])
                                except StopIteration:
                                    active.remove(ent); break

                def qk_prep(src_ps, nh, gofs, xfb, bfbuf, gtile, sqt, ss, rt):
                    n = nh * 64
                    xf = xfb[:, 0:n]
                    COPY('act', xf, src_ps); yield
                    TT('pool', sqt[:, 0:n], xf, xf, ALU.mult); yield
                    SC.op('dve', lambda e: e.tensor_reduce(out=ss, in_=sqt[:, 0:n].rearrange("p (h d) -> p h d", h=nh), axis=AX.X, op=ALU.add),
                          r=[sqt[:, 0:n]], w=[ss]); yield
                    ACT(ss, ss, AF.Ln, scale=1.0 / 64, bias=epsc[:, 0:1], r=[ss, epsc[:, 0:1]]); yield
                    ACT(ss, ss, AF.Exp, scale=-0.5); yield
                    x3 = xf.rearrange("p (h d) -> p h d", h=nh)
                    TT('dve', x3, x3, ss.unsqueeze(2).to_broadcast([128, nh, 64]), ALU.mult, r=[xf, ss], w=[xf]); yield
                    TT('dve', x3, x3, prow[:, gofs:gofs + 64].unsqueeze(1).to_broadcast([128, nh, 64]), ALU.mult,
                       r=[xf, prow[:, gofs:gofs + 64]], w=[xf]); yield
                    x1 = x3[:, :, 0:8]; x2 = x3[:, :, 8:16]
                    cb = cosT[:, gtile, :].unsqueeze(1).to_broadcast([128, nh, 8])
                    sb_ = sinT[:, gtile, :].unsqueeze(1).to_broadcast([128, nh, 8])
                    w8 = nh * 8
                    tt = [rt[:, i * w8:(i + 1) * w8].rearrange("p (h d) -> p h d", h=nh) for i in range(4)]
                    tr_ = [rt[:, 0:4 * w8]]
                    cr = [cosT[:, gtile, :]]; sr = [sinT[:, gtile, :]]
                    TT('pool', tt[0], x1, cb, ALU.mult, r=[xf] + cr, w=tr_); yield
                    TT('pool', tt[1], x2, sb_, ALU.mult, r=[xf] + sr, w=tr_); yield
                    TT('pool', tt[2], x2, cb, ALU.mult, r=[xf] + cr, w=tr_); yield
                    TT('pool', tt[3], x1, sb_, ALU.mult, r=[xf] + sr, w=tr_); yield
                    TT('pool', x1, tt[0], tt[1], ALU.subtract, r=tr_, w=[xf]); yield
                    TT('pool', x2, tt[2], tt[3], ALU.add, r=tr_, w=[xf]); yield
                    COPY('act', bfbuf[:, 0:n], xf); yield

                def kv_tile(j, par):
                    gt = 4 * it + j
                    slot = gt % 8
                    yield from qk_prep(kvb[j][:, 0:128], 2, PR_KG, k_fs[par], k_bfs[par], gt, k_sq[par], stat[:, 2 * par:2 * par + 2], rtmps[par])
                    tb = bank(); tbb = tb[:].bitcast(BF16)
                    TRANS([tbb[0:64, h * 128:(h + 1) * 128] for h in range(2)], [k_bfs[par][:, h * 64:(h + 1) * 64] for h in range(2)],
                          r=[k_bfs[par]], w=[tbb[0:64, 0:256]]); yield
                    SC.op('dve', lambda e, tbb=tbb, slot=slot: e.tensor_copy(out=kT[0:64, :, slot, :], in_=tbb[0:64, 0:256].rearrange("p (h t) -> p h t", h=2)),
                          r=[tbb[0:64, 0:256]], w=[kT[0:64, h, slot, :] for h in range(2)]); rel(tb); yield
                    COPY('act', vaug[:, slot, :, 0:64], kvb[j][:, 128:256].rearrange("p (h d) -> p h d", h=2)); yield
                def do_ktiles(ktaps):
                    for i2 in range(0, len(kvt), 2):
                        merge(*([(kv_tile(j, p), 1) for p, j in enumerate(kvt[i2:i2 + 2])] + ([(taps_n(7), 1)] if ktaps else [])))
                    rel(*kvbanks)

                if stage == 2: SC.dead = True
                def do_convproj():
                    Nc = min(512, S - (t0 + 16))
                    cvb = []; cvx = []; cvxb = []
                    for c in range(4):
                        wv, wr = WGET(('w_in', 0, 1280 + 128 * c))
                        b = bank(); cvb.append(b)
                        MM(b[:, 0:Nc], [(wv[:, kc, :], hT[:, kc, 16:16 + Nc]) for kc in range(8)])
                        if it == 0:
                            if c % 2 == 0:
                                bx = bank(); cvxb.append(bx)
                            bxs = bx[:, (c % 2) * 16:(c % 2) * 16 + 16]
                            cvx.append(bxs)
                            MM(bxs, [(wv[:, kc, :], hT[:, kc, 0:16]) for kc in range(8)])
                    if it == 0:
                        MEMSET('pool', glu[:, :, 0:16], 0.0)
                    else:
                        COPY('pool', glu[:, :, 0:32], gtail)
                    if Nc < 512:
                        MEMSET('pool', glu[:, :, 32 + Nc:544], 0.0)
                    for c in range(2):
                        ACT(sig[:, 0:Nc], cvb[2 + c][:, 0:Nc], AF.Sigmoid)
                        TT('dve', glu[:, c, 32:32 + Nc], cvb[c][:, 0:Nc], sig[:, 0:Nc], ALU.mult)
                        if it == 0:
                            ACT(stat[:, 16:32], cvx[2 + c], AF.Sigmoid)
                            TT('dve', glu[:, c, 16:32], cvx[c], stat[:, 16:32], ALU.mult)
                    COPY('pool', gtail, glu[:, :, 512:544])
                    rel(*cvb); rel(*cvxb)
                if it == 0:
                    do_ktiles(False); do_convproj()
                else:
                    do_convproj()
                def taps_gen():
                    for k in range(31):
                        for c in range(2):
                            cw = PC_CW + 31 * c
                            if k == 0:
                                TS('dve', acc[:, c, :], glu[:, c, 1:513], pcols[:, cw:cw + 1], pcols[:, PC_CB + c:PC_CB + c + 1], ALU.mult, ALU.add)
                            else:
                                STT(acc[:, c, :], glu[:, c, 1 + k:513 + k], pcols[:, cw + k:cw + k + 1], acc[:, c, :], ALU.mult, ALU.add)
                            yield
                taps = taps_gen()
                def BG(n):
                    for _ in range(n):
                        try: next(taps)
                        except StopIteration: return
                def taps_n(n):
                    for _ in range(n):
                        try: next(taps)
                        except StopIteration: return
                        yield

                if it > 0: do_ktiles(True)
                def conv_finish():
                    for _ in taps: yield
                    b1 = bank(); b2 = bank()
                    for c in range(2):
                        COPY('act', ybf[:, c, :], acc[:, c, :]); yield
                        MM(b1, [(ones, ybf[:, c, :])], first=(c == 0), last=(c == 1)); yield
                    for c in range(2):
                        ACT(sqb[:, c, 0:512], acc[:, c, :], AF.Square); yield
                        MM(b2, [(ones, sqb[:, c, 0:512])], first=(c == 0), last=(c == 1)); yield
                    rc = rstd_b[:, 0:512]
                    TS('dve', sig, b1, 1.0 / 256, None, ALU.mult); yield
                    TS('dve', rc, b2, 1.0 / 256, None, ALU.mult); rel(b1, b2); yield
                    msq = ybf_f.bitcast(F32)[:, 0:512]
                    TT('pool', msq, sig, sig, ALU.mult, r=[sig], w=[ybf_f]); yield
                    TT('dve', rc, rc, msq, ALU.subtract, r=[rc, ybf_f], w=[rc]); yield
                    ACT(rc, rc, AF.Ln, scale=1.0, bias=epsc[:, 0:1], r=[rc, epsc[:, 0:1]]); yield
                    ACT(rc, rc, AF.Exp, scale=-0.5); yield
                    for c in range(2):
                        TT('dve', acc[:, c, :], acc[:, c, :], sig, ALU.subtract); yield
                        TT('dve', acc[:, c, :], acc[:, c, :], rc, ALU.mult); yield
                        ACT(acc[:, c, :], acc[:, c, :], AF.Silu, scale=pcols[:, PC_CLG + c:PC_CLG + c + 1], bias=pcols[:, PC_CLB + c:PC_CLB + c + 1],
                            r=[acc[:, c, :], pcols[:, PC_CLG:PC_CLB + 2]]); yield
                    b3 = bank()
                    for c in range(2):
                        ACT(sqb[:, c, 0:512], acc[:, c, :], AF.Square); yield
                        MM(b3, [(ones, sqb[:, c, 0:512])], first=(c == 0), last=(c == 1)); yield
                    ACT(rc, b3, AF.Ln, scale=1.0 / 256, bias=epsc[:, 0:1], r=[b3, epsc[:, 0:1]]); rel(b3); yield
                    ACT(rc, rc, AF.Exp, scale=-0.5); yield
                    for c in range(2):
                        STT(mergedT[:, 6 + c, :], acc[:, c, :], pcols[:, PC_OGC + c:PC_OGC + c + 1], rc, ALU.mult, ALU.mult); yield

                if stage == 3: SC.dead = True
                gvb = {}; gvbanks = []
                for j in range(4):
                    if j % 2 == 0:
                        bcur = bank(); gvbanks.append(bcur)
                    gvb[j] = bcur[:, (j % 2) * 256:(j % 2) * 256 + 256]
                WHOLD()
                wgv = [WGET(('w_in', 4 * s2, 1024))[0] for s2 in range(2)]
                for j in range(4):
                    MM(gvb[j], [(hT[:, 4 * s2 + kk, j * 128:(j + 1) * 128], wgv[s2][:, kk, :]) for s2 in range(2) for kk in range(4)])
                WRELEASE()
                gub = []
                for c in range(2):
                    wv, wr = WGET(('w_in', 0, 768 + 128 * c))
                    b = bank(); gub.append(b)
                    MM(b, [(wv[:, kc, :], hT[:, kc, 0:512]) for kc in range(8)])
                for c in range(2):
                    ACT(uT[:, c, :], gub[c], AF.Gelu)
                rel(*gub)
                def gm_tile(j, par):
                    g_ = gvs[par]; vb_ = vn_bfs[par]; gt_ = gtmps[par]
                    ACT(g_[:, 0:256], gvb[j], AF.Gelu); yield
                    st6 = stat[:, 32 + 16 * par:38 + 16 * par]; mv = stat[:, 40 + 16 * par:42 + 16 * par]
                    SC.op('dve', lambda e: e.bn_stats(out=st6, in_=g_[:, 0:256]), r=[g_], w=[st6]); yield
                    SC.op('dve', lambda e: e.bn_aggr(out=mv, in_=st6), r=[st6], w=[mv]); yield
                    rs = stat[:, 44 + 16 * par:45 + 16 * par]
                    ACT(rs, mv[:, 1:2], AF.Ln, scale=1.0, bias=epsc[:, 0:1], r=[mv, epsc[:, 0:1]]); yield
                    ACT(rs, rs, AF.Exp, scale=-0.5); yield
                    TS('dve', g_[:, 0:256], g_[:, 0:256], mv[:, 0:1], rs, ALU.subtract, ALU.mult); yield
                    TT('dve', g_[:, 0:256], g_[:, 0:256], prow[:, PR_LNG:PR_LNG + 256], ALU.mult); yield
                    TT('dve', vb_[:, 0:256], g_[:, 0:256], prow[:, PR_LNB:PR_LNB + 256], ALU.add); yield
                    sgb = bank()
                    for c in range(2):
                        MM(sgb[:, c * 256:(c + 1) * 256], [(vb_[:, c * 128:(c + 1) * 128], wsT[:, 2 * c:2 * c + 2, :])]); yield
                    for c in range(2):
                        for hh in range(2):
                            ps_ = slice(64 * hh, 64 * hh + 64)
                            src = sgb[ps_, c * 256 + hh * 128:c * 256 + hh * 128 + 128]
                            TT('dve', gt_[ps_, c * 128:c * 128 + 128], src, bsT[ps_, c, :], ALU.add); yield
                            TT('pool', uT[ps_, c, j * 128:(j + 1) * 128], uT[ps_, c, j * 128:(j + 1) * 128], gt_[ps_, c * 128:c * 128 + 128], ALU.mult); yield
                    rel(sgb)
                merge((gm_tile(0, 0), 1), (gm_tile(1, 1), 1), (taps_n(12), 1))
                merge((gm_tile(2, 0), 1), (gm_tile(3, 1), 1), (taps_n(12), 1))
                rel(*gvbanks)
                b4 = bank()
                for c in range(2):
                    ACT(sqb[:, c, 0:512], uT[:, c, :], AF.Square)
                    MM(b4, [(ones, sqb[:, c, 0:512])], first=(c == 0), last=(c == 1))
                RSQ(rstd_b[:, 0:512], b4, 1.0 / 256); rel(b4)
                for c in range(2):
                    STT(mergedT[:, 4 + c, :], uT[:, c, :], pcols[:, PC_OGG + c:PC_OGG + c + 1], rstd_b[:, 0:512], ALU.mult, ALU.mult)

                if stage == 4: SC.dead = True
                WHOLD()
                qw = [WGET(('w_in', 2 * s4, 0))[0] for s4 in range(4)]
                def att_prep(j):
                    gt = 4 * it + j; par = j % 2
                    qbj = bank()
                    MM(qbj, [(hT[:, 2 * s4 + kk, j * 128:(j + 1) * 128], qw[s4][:, kk, :]) for s4 in range(4) for kk in range(2)]); yield
                    first = True
                    for _ in qk_prep(qbj, 8, PR_QG, q_fs[par], q_bfs[par], gt, sqtmps[par], stat[:, 8 + 8 * par:16 + 8 * par], sqtmps[par]):
                        if first: rel(qbj); first = False
                        yield
                    tb = bank(); tbb = tb[:].bitcast(BF16)
                    TRANS([tbb[0:64, h * 128:(h + 1) * 128] for h in range(8)], [q_bfs[par][:, h * 64:(h + 1) * 64] for h in range(8)],
                          r=[q_bfs[par]], w=[tbb[0:64, :]]); yield
                    COPY('dve', qT_sbs[par][0:64, :], tbb[0:64, :]); rel(tb); yield
                obs = {}
                def att_A(j):
                    gt = 4 * it + j; par = j % 2
                    qT_sb = qT_sbs[par]
                    blocks = [b for b in (gt - 1, gt, gt + 1) if 0 <= b < 32]
                    for kvh in range(2):
                        for b in blocks:
                            bi = kvh * 3 + (b - gt + 1)
                            sb2 = bank()
                            pairs = [(kT[0:64, kvh, b % 8, :], qT_sb[0:64, kvh * 512:(kvh + 1) * 512])]
                            if b != gt:
                                pairs.append((ident, negp if b < gt else negn))
                            MM(sb2, pairs); yield
                            ACT(PT[:, bi, :], sb2, AF.Exp, scale=0.125); rel(sb2); yield
                    ob = [bank(), bank()]; obs[j] = ob
                    for h in range(8):
                        kvh, hq = divmod(h, 4)
                        outp = ob[h // 4][:, (h % 4) * 65:(h % 4) * 65 + 65]
                        MM(outp, [(PT[:, kvh * 3 + (b - gt + 1), hq * 128:(hq + 1) * 128], vaug[:, b % 8, kvh, :]) for b in blocks]); yield
                def att_B(j):
                    par = j % 2
                    ob = obs[j]
                    den = stat[:, 64 + 8 * par:72 + 8 * par]
                    for hb in range(2):
                        o3 = ob[hb][:, 0:260].rearrange("p (h d) -> p h d", h=4)
                        TT('dve', den[:, hb * 4:hb * 4 + 4], o3[:, :, 64], esink[:, hb * 4:hb * 4 + 4], ALU.add, r=[ob[hb][:, 0:260], esink], w=[den]); yield
                    SC.op('dve', lambda e, den=den: e.reciprocal(out=den, in_=den), r=[den], w=[den]); yield
                    for hb in range(2):
                        o3 = ob[hb][:, 0:260].rearrange("p (h d) -> p h d", h=4)
                        TT('dve', o_f[:, hb * 256:(hb + 1) * 256].rearrange("p (h d) -> p h d", h=4), o3[:, :, 0:64],
                           den[:, hb * 4:hb * 4 + 4].unsqueeze(2).to_broadcast([128, 4, 64]), ALU.mult,
                           r=[ob[hb][:, 0:260], den], w=[o_f[:, hb * 256:(hb + 1) * 256]]); yield
                    rel(*ob)
                    ss = stat[:, 80 + par:81 + par]
                    ACT(m_bf, o_f, AF.Square, accum_out=ss, w=[m_bf, ss]); yield
                    ACT(ss, ss, AF.Ln, scale=1.0 / 512, bias=epsc[:, 0:1], r=[ss, epsc[:, 0:1]]); yield
                    ACT(ss, ss, AF.Exp, scale=-0.5); yield
                    TS('dve', m_bf, o_f, ss, None, ALU.mult); yield
                    tb2 = bank(); tbb2 = tb2[:].bitcast(BF16)
                    TRANS([tbb2[:, c * 128:(c + 1) * 128] for c in range(4)], [m_bf[:, c * 128:(c + 1) * 128] for c in range(4)],
                          r=[m_bf], w=[tbb2[:, 0:512]]); yield
                    for c in range(4):
                        ACT(mergedT[:, c, j * 128:(j + 1) * 128], tbb2[:, c * 128:(c + 1) * 128], AF.Identity, scale=pcols[:, PC_OGA + c:PC_OGA + c + 1],
                            r=[tbb2[:, c * 128:(c + 1) * 128], pcols[:, PC_OGA:PC_OGA + 4]]); yield
                    rel(tb2)
                merge((att_prep(0), 1), (taps_n(6), 1))
                merge((att_prep(1), 1), (att_A(0), 1), (taps_n(10), 1))
                for j in range(4):
                    gens = [(att_B(j), 1)]
                    if j + 1 < 4: gens.append((att_A(j + 1), 2))
                    if j + 2 < 4: gens.append((att_prep(j + 2), 1))
                    if j == 0: gens.append((taps_n(10), 1))
                    if j == 1: gens.append((conv_finish(), 2))
                    merge(*gens)

                if stage == 5: SC.dead = True
                WRELEASE()
                for m in range(8):
                    wv, wr = WGET(('w_out', 0, 128 * m))
                    b = bank()
                    MM(b, [(wv[:, kc, :], mergedT[:, kc, :]) for kc in range(8)])
                    TT('dve', X[:, m, t0:t0 + T], b, X[:, m, t0:t0 + T], ALU.add); rel(b)
                    if m == 0:
                        pcsF = norm_pieces(T); bksF = [bank() for _ in pcsF]
                    else:
                        norm_stats(m - 1, T, t0, bksF, pcsF)
                norm_stats(7, T, t0, bksF, pcsF)
                norm_fin(bksF, pcsF, rstd_b)
                ACT(stat[:, 100:101], stat[:, 100:101], AF.Silu)
                norm_apply(PC_FFN, T, t0, rstd_b)

                if stage == 6: SC.dead = True
                for (f0, f1) in ((0, 16), (16, 22)):
                    for f in range(f0, f1):
                        wg, _ = WGET(('w_gate_up', 0, 128 * f))
                        bg = bank()
                        MM(bg, [(wg[:, kc, :], hT[:, kc, 0:T]) for kc in range(8)], split=(f == 0))
                        wu, _ = WGET(('w_gate_up', 0, DFF + 128 * f))
                        bu = bank()
                        MM(bu, [(wu[:, kc, :], hT[:, kc, 0:T]) for kc in range(8)])
                        sgt = sg2[f % 2]
                        ACT(sgt, bg, AF.Silu); rel(bg)
                        TT('dve', actb[:, f - f0, :], bu, sgt, ALU.mult); rel(bu)
                        if nxt_t0 is not None and f < 8:
                            Wn_ = min(640, S - nxt_t0)
                            if f == 0:
                                pcsN = norm_pieces(Wn_); bksN = [bank() for _ in pcsN]
                            norm_stats(f, Wn_, nxt_t0, bksN, pcsN)
                            if f == 7:
                                norm_fin(bksN, pcsN, rstd_n); prestat[0] = True
                    for m in range(8):
                        b = bank()
                        pieces = list(range(f0, f1, 8))
                        for pi, k0 in enumerate(pieces):
                            nk = min(8, f1 - k0)
                            wv, _ = WGET(('w_down', k0, 128 * m))
                            MM(b, [(wv[:, kk, :], actb[:, k0 - f0 + kk, :]) for kk in range(nk)],
                               first=(pi == 0), last=(pi == len(pieces) - 1))
                        TT('dve', X[:, m, t0:t0 + T], b, X[:, m, t0:t0 + T], ALU.add); rel(b)
                        if f0 == 16:
                            if m == 0:
                                pcsP = norm_pieces(T); bksP = [bank() for _ in pcsP]
                            else:
                                norm_stats(m - 1, T, t0, bksP, pcsP)

                if stage == 7: SC.dead = True
                norm_stats(7, T, t0, bksP, pcsP)
                norm_fin(bksP, pcsP, rstd_b)
                ACT(stat[:, 101:102], stat[:, 101:102], AF.Sigmoid)
                norm_apply(PC_PLE, T, t0, rstd_b)
                pv = pTd[l].rearrange("(c p) t -> p c t", p=128)
                SC.dma('pool', 'pl', lambda e, s, pv=pv, t0=t0: e.dma_start(out=pTb, in_=pv[:, :, t0:t0 + T]).then_inc(s, 16), w=[pTb_f])
                for m in range(8):
                    if m % 4 == 0:
                        WHOLD()
                        pjw, _ = WGET(('w_ple_proj', 0, 512 * (m // 4)))
                    wg, _ = WGET(('w_ple_gate', 0, 128 * m))
                    bg = bank()
                    MM(bg, [(wg[:, kc, :], hT[:, kc, 0:T]) for kc in range(8)], split=(m == 0))
                    bp = bank()
                    MM(bp, [(pjw[:, kk, (m % 4) * 128:(m % 4) * 128 + 128], pTb[:, kk, :]) for kk in range(2)])
                    if m % 4 == 3: WRELEASE()
                    sgt = sg2[m % 2]
                    ACT(sgt, bg, AF.Sigmoid); rel(bg)
                    TT('dve', ptmp[m % 2], bp, sgt, ALU.mult); rel(bp)
                    TT('pool', X[:, m, t0:t0 + T], X[:, m, t0:t0 + T], ptmp[m % 2], ALU.add)
                SC.dead = False; SC.maxrec = 10**9
                ACT(stat[:, 102:103], stat[:, 101:102], AF.Exp)
                if li == len(layers) - 1:
                    SC.dma('pool', 'out', lambda e, s, t0=t0: e.dma_start(out=oTv[:, :, t0:t0 + T], in_=X[:, :, t0:t0 + T]).then_inc(s, 16),
                           r=[X[:, kc, t0:t0 + T] for kc in range(8)])
        SC.wait('sp', ('out', SC.cnt['out']))
        SC.wait('pool', ('out', SC.cnt['out']))
        for s in [f'wst{i}' for i in range(4)]:
            if SC.cnt[s]: SC.wait('sp', (s, SC.cnt[s]))
        SC.replay(block)
    return nc


def _host_layout(inp):
    f32 = np.float32
    g = {k: np.asarray(v) for k, v in inp.items()}
    L = DEPTH
    pcols = np.zeros((L, 128, NPC), f32)
    prow = np.zeros((L, 1, NPR), f32)
    for l in range(L):
        def col(v, n): return np.ascontiguousarray(v.reshape(n, 128).T)
        pcols[l, :, PC_MIX:PC_MIX + 8] = col(g['norm_mix_g'][l], 8)
        pcols[l, :, PC_FFN:PC_FFN + 8] = col(g['norm_ffn_g'][l], 8)
        pcols[l, :, PC_PLE:PC_PLE + 8] = col(g['ple_norm_g'][l], 8)
        og = g['out_norm_g'][l]
        pcols[l, :, PC_OGA:PC_OGA + 4] = col(og[0:512], 4)
        pcols[l, :, PC_OGG:PC_OGG + 2] = col(og[512:768], 2)
        pcols[l, :, PC_OGC:PC_OGC + 2] = col(og[768:1024], 2)
        cw = g['conv_w'][l]
        for c in range(2):
            pcols[l, :, PC_CW + 31 * c:PC_CW + 31 * c + 31] = cw[:, c * 128:(c + 1) * 128].T
        pcols[l, :, PC_CB:PC_CB + 2] = col(g['conv_b'][l], 2)
        pcols[l, :, PC_CLG:PC_CLG + 2] = col(g['conv_ln_g'][l], 2)
        pcols[l, :, PC_CLB:PC_CLB + 2] = col(g['conv_ln_b'][l], 2)
        prow[l, 0, PR_QG:PR_QG + 64] = g['q_norm_g'][l]
        prow[l, 0, PR_KG:PR_KG + 64] = g['k_norm_g'][l]
        prow[l, 0, PR_SINK:PR_SINK + 8] = g['sink'][l]
        prow[l, 0, PR_LNG:PR_LNG + 256] = g['gm_ln_g'][l]
        prow[l, 0, PR_LNB:PR_LNB + 256] = g['gm_ln_b'][l]
    wsT = np.ascontiguousarray(g['gm_ws'].transpose(0, 3, 1, 2)).reshape(L, 128, 512).astype(f32)
    consts = np.zeros((128, 392), f32)
    consts[:, 0:128] = np.eye(128, dtype=f32)
    jj = np.arange(128)[:, None]; ii = np.arange(128)[None, :]
    consts[:, 128:256] = (jj >= ii).astype(f32)
    consts[:, 256:384] = (jj <= ii).astype(f32)
    consts[:, 384:392] = (500000.0 ** (-np.arange(0, 16, 2, dtype=f32) / 16)).astype(f32)[None, :]
    common = dict(consts=consts, pcols=pcols, prow=prow, gm_bs=np.ascontiguousarray(g['gm_bs'], f32), wsT=wsT)
    for k in ('w_in', 'w_out', 'w_gate_up', 'w_down', 'w_ple_gate', 'w_ple_proj'):
        common[k] = np.ascontiguousarray(g[k], f32)
    maps = []
    for c in range(NCORES):
        m = dict(common)
        m['xT'] = np.ascontiguousarray(g['x'][c].T)
        m['pT'] = np.ascontiguousarray(g['p'][:, c].transpose(0, 2, 1))
        m['pos'] = np.ascontiguousarray(g['positions'][c].reshape(32, 128).T).astype(np.int32)
        maps.append(m)
    return maps


_NC_CACHE = {}


def kernel(**inputs):
    maps = _host_layout(inputs)
    key = 'fused'
    if key not in _NC_CACHE:
        _NC_CACHE[key] = build_nc([0, 1], True)
    nc = _NC_CACHE[key]
    res = run_bass_kernel_spmd(nc, maps, core_ids=list(range(NCORES)))
    out = np.stack([np.ascontiguousarray(res.results[c]["oT"].T) for c in range(NCORES)], axis=0)
    return out.astype(np.float32)
```

```python
import math
from contextlib import ExitStack
import numpy as np
import concourse.bass as bass
import concourse.mybir as mybir
from concourse.bass_utils import run_bass_kernel_spmd

F32 = mybir.dt.float32; BF16 = mybir.dt.bfloat16; I32 = mybir.dt.int32
AF = mybir.ActivationFunctionType; ALU = mybir.AluOpType; AX = mybir.AxisListType

NCORES = 8
D = 1024; S = 4096; DEPTH = 2; T = 512; NG = S // T
DFF = 2816; INC = 1792; PLE = 256
EPS = 1e-6
NSLOT = 8
CELL = 64


def _box(ap):
    es = mybir.dt.size(ap.dtype)
    a = ap.ap
    ps = a[0][0]
    off = ap.offset
    fo = off % ps if ps > 0 else off
    ext = 0
    for (st, cn) in a[1:]:
        ext += abs(st) * (cn - 1)
    lo = fo * es
    hi = (fo + ext + 1) * es
    if ap.tensor.name.startswith('bk'):
        return (ap.tensor.name, 0, 0)
    return (ap.tensor.name, lo // CELL, (hi - 1) // CELL)


class Sched:
    ENG = ['pe', 'act', 'dve', 'pool', 'sp']

    def __init__(self, nc, sems):
        self.nc = nc
        self.sem = sems
        self.cnt = {k: 0 for k in sems}
        self.prog = {e: [] for e in self.ENG}
        self.seen = {e: {} for e in self.ENG}
        self.cells = {}
        self.last = {}
        self.dead = False
        self.nrec = 0
        self.maxrec = 10**9
        self.log = []

    def _deps(self, reads, writes):
        deps = {}
        for ap in reads:
            n, lo, hi = _box(ap)
            for c in range(lo, hi + 1):
                st = self.cells.get((n, c))
                if st and st[0]:
                    s, v = st[0]
                    if deps.get(s, 0) < v: deps[s] = v
        for ap in writes:
            n, lo, hi = _box(ap)
            for c in range(lo, hi + 1):
                st = self.cells.get((n, c))
                if st:
                    if st[0]:
                        s, v = st[0]
                        if deps.get(s, 0) < v: deps[s] = v
                    for s, v in st[1].items():
                        if deps.get(s, 0) < v: deps[s] = v
        return deps

    def _emit_waits(self, eng, deps):
        for s, v in deps.items():
            if s == 'pe' and eng == 'pe': continue
            if self.seen[eng].get(s, 0) >= v: continue
            self.seen[eng][s] = v
            self.prog[eng].append(('wait', s, v))

    def _mark(self, reads, writes, tick):
        s, v = tick
        for ap in reads:
            n, lo, hi = _box(ap)
            for c in range(lo, hi + 1):
                st = self.cells.setdefault((n, c), [None, {}])
                if st[1].get(s, 0) < v: st[1][s] = v
        for ap in writes:
            n, lo, hi = _box(ap)
            for c in range(lo, hi + 1):
                self.cells[(n, c)] = [tick, {}]

    def op(self, eng, fn, r=(), w=(), extra=()):
        if self.dead: return ('pe', 0)
        self.nrec += 1
        if self.nrec > self.maxrec: return ('pe', 0)
        import traceback
        self.log.append((self.nrec, eng, traceback.extract_stack()[-3].lineno, traceback.extract_stack()[-2].lineno))
        deps = self._deps(r, w)
        for sv in extra:
            if sv is not None and deps.get(sv[0], 0) < sv[1]: deps[sv[0]] = sv[1]
        self._emit_waits(eng, deps)
        self.cnt[eng] += 1
        tick = (eng, self.cnt[eng])
        self.prog[eng].append(('ins', fn, eng))
        self._mark(r, w, tick)
        return tick

    def dma(self, q, semname, fn, r=(), w=(), extra=()):
        if self.dead: return ('pe', 0)
        deps = self._deps(r, w)
        for sv in list(extra) + [self.last.get(semname)]:
            if sv is not None and deps.get(sv[0], 0) < sv[1]: deps[sv[0]] = sv[1]
        self._emit_waits(q, deps)
        self.cnt[semname] += 16
        tick = (semname, self.cnt[semname])
        self.last[semname] = tick
        self.prog[q].append(('dma', fn, semname))
        self._mark(r, w, tick)
        return tick

    def wait(self, eng, tick):
        self._emit_waits(eng, {tick[0]: tick[1]})

    def replay(self, block):
        engs = {'pe': block.tensor, 'act': block.scalar, 'dve': block.vector,
                'pool': block.gpsimd, 'sp': block.sync}
        for en, deco in engs.items():
            prog = self.prog[en]

            def body(e, prog=prog):
                for it in prog:
                    if it[0] == 'wait':
                        e.wait_ge(self.sem[it[1]], it[2])
                    elif it[0] == 'ins':
                        it[1](e).then_inc(self.sem[it[2]], 1)
                    else:
                        it[1](e, self.sem[it[2]])
            deco(body)


def slab_table():
    t = []
    for s in range(2): t.append(('w_in', 4 * s, 4, 512, 256))
    for c in range(4): t.append(('w_in', 0, 8, 1280 + 128 * c, 128))
    for s in range(2): t.append(('w_in', 4 * s, 4, 1024, 256))
    for c in range(2): t.append(('w_in', 0, 8, 768 + 128 * c, 128))
    for s in range(4): t.append(('w_in', 2 * s, 2, 0, 512))
    for m in range(8): t.append(('w_out', 0, 8, 128 * m, 128))
    for (f0, f1) in ((0, 16), (16, 22)):
        for f in range(f0, f1):
            t.append(('w_gate_up', 0, 8, 128 * f, 128))
            t.append(('w_gate_up', 0, 8, DFF + 128 * f, 128))
        for m in range(8):
            for k0 in range(f0, f1, 8):
                t.append(('w_down', k0, min(8, f1 - k0), 128 * m, 128))
    for s in range(2):
        t.append(('w_ple_proj', 0, 2, 512 * s, 512))
        for m in range(4 * s, 4 * s + 4): t.append(('w_ple_gate', 0, 8, 128 * m, 128))
    return t


NPC = 104
PC_MIX, PC_FFN, PC_PLE, PC_OGA, PC_OGG, PC_OGC, PC_CW, PC_CB, PC_CLG, PC_CLB = 0, 8, 16, 24, 28, 30, 32, 94, 96, 98
NPR = 648
PR_QG, PR_KG, PR_SINK, PR_LNG, PR_LNB = 0, 64, 128, 136, 392


def build_nc(layers, last, ngrun=NG, stage=99):
    import os as _os2
    nc = bass.Bass("TRN2", target_bir_lowering=False, dynamic_dma_scratch_size=int(_os2.environ.get("KSCR", 4096)))
    dram = {}
    def din(name, shape, dt=F32):
        dram[name] = nc.dram_tensor(name, shape, dt, kind="ExternalInput").ap()
        return dram[name]
    xT = din("xT", [D, S]); pTd = din("pT", [DEPTH, PLE, S]); posd = din("pos", [128, 32], I32)
    constd = din("consts", [128, 392]); pcolsd = din("pcols", [DEPTH, 128, NPC]); prowd = din("prow", [DEPTH, 1, NPR])
    bsd = din("gm_bs", [DEPTH, 4, 128]); wsd = din("wsT", [DEPTH, 128, 512])
    wd = {}
    for name, shp in (("w_in", [DEPTH, D, INC]), ("w_out", [DEPTH, D, D]), ("w_gate_up", [DEPTH, D, 2 * DFF]),
                      ("w_down", [DEPTH, DFF, D]), ("w_ple_gate", [DEPTH, D, D]), ("w_ple_proj", [DEPTH, PLE, D])):
        wd[name] = din(name, shp)
    oT = nc.dram_tensor("oT", [D, S], F32, kind="ExternalOutput").ap()
    slabs = slab_table(); NS = len(slabs)
    scr = nc.dram_tensor("wscr", [DEPTH * NS, 128, 1024], BF16, kind="Internal").ap()

    with ExitStack() as es:
        X = es.enter_context(nc.sbuf_tensor("X", [128, 8, S], F32))
        ARENA_F = (81664 + 16384 - int(_os2.environ.get('KSCR', 4096))) // 4
        A = es.enter_context(nc.sbuf_tensor("A", [128, ARENA_F], F32))
        banks = [es.enter_context(nc.psum_tensor(f"bk{i}", [128, 512], F32)) for i in range(8)]
        semnames = ['pe', 'act', 'dve', 'pool', 'sp', 'xl0', 'xl1', 'par', 'pl', 'st', 'out'] + \
                   [f'ws{i}' for i in range(NSLOT)] + [f'wq{i}' for i in range(NSLOT)] + [f'wst{i}' for i in range(4)] + [f'p{i}' for i in range(10)]
        sems = {n: es.enter_context(nc.semaphore(n)) for n in semnames}
        block = es.enter_context(nc.Block())
        SC = Sched(nc, sems)
        import os
        SC.maxrec = int(os.environ.get('KMAX', 10**9))
        build_nc.SC = SC

        cur = [0]
        def carve(nbytes, dt=F32, shape=None, at=None):
            if at is None:
                off = cur[0]; cur[0] += (nbytes + 63) // 64 * 64
            else:
                off = at
            assert off % 4 == 0 and off + nbytes <= ARENA_F * 4, (off, nbytes)
            v = A[:, off // 4:(off + nbytes + 3) // 4]
            if dt != F32: v = v.bitcast(dt)
            return v, off
        def view3(v, pat, **kw): return v.rearrange(pat, **kw)

        hT_f, _ = carve(8 * 640 * 2, BF16); hT = hT_f.rearrange("p (k t) -> p k t", k=8)
        ring = [carve(2048, BF16)[0] for _ in range(NSLOT)]
        mT_f, _ = carve(8 * 512 * 2, BF16); mergedT = mT_f.rearrange("p (k t) -> p k t", k=8)
        rstd_b, _ = carve(640 * 4)
        sq_f, _ = carve(2 * 640 * 2, BF16); sqb = sq_f.rearrange("p (k t) -> p k t", k=2)
        ident, _ = carve(256, BF16); ones, _ = carve(256, BF16)
        negp, _ = carve(1024, BF16); negn, _ = carve(1024, BF16)
        cos_f, _ = carve(1024); sin_f, _ = carve(1024)
        cosT = cos_f.rearrange("p (n j) -> p n j", j=8); sinT = sin_f.rearrange("p (n j) -> p n j", j=8)
        mhalf, _ = carve(64)
        pcols, _ = carve(NPC * 4); prow, _ = carve(NPR * 4)
        bsT_f, _ = carve(1024); bsT = bsT_f.rearrange("p (c q) -> p c q", c=2)
        wsT_f, _ = carve(1024, BF16); wsT = wsT_f.rearrange("p (h q) -> p h q", h=4)
        esink, _ = carve(64)
        kT_f, _ = carve(4096, BF16); kT = kT_f.rearrange("p (h s t) -> p h s t", h=2, s=8)
        va_f, _ = carve(8 * 2 * 65 * 2 + 32, BF16); vaug = va_f[:, 0:8 * 2 * 65].rearrange("p (s h d) -> p s h d", s=8, h=2)
        gtail_f, _ = carve(256); gtail = gtail_f.rearrange("p (c j) -> p c j", c=2)
        stat, _ = carve(512)
        U0 = cur[0]
        USZ = ARENA_F * 4 - U0
        assert USZ >= 36096, USZ
        def ucarve(off, nbytes, dt=F32): return carve(nbytes, dt, at=U0 + off)[0]
        rstd_n = ucarve(22528, 640 * 4)
        pTb_f = ucarve(20480, 2048, BF16); pTb = pTb_f.rearrange("p (c t) -> p c t", c=2)
        act_f = ucarve(0, 16 * 512 * 2, BF16); actb = act_f.rearrange("p (k t) -> p k t", k=16)
        sg2 = [ucarve(16384, 2048), ucarve(18432, 2048)]
        ptmp = [ucarve(0, 2048), ucarve(2048, 2048)]
        q_fs = [ucarve(0, 2048), ucarve(2048, 2048)]
        sqtmps = [ucarve(4096, 2048), ucarve(18432, 2048)]
        q_bfs = [ucarve(6144, 1024, BF16), ucarve(7168, 1024, BF16)]
        qT_sbs = [ucarve(8192, 2048, BF16), ucarve(10240, 2048, BF16)]
        PT_f = ucarve(12288, 6144, BF16); PT = PT_f.rearrange("p (i t) -> p i t", i=6)
        sqtmps[1] = ucarve(20480, 2048)
        o_f = ucarve(18432, 2048)
        k_fs = [ucarve(12288, 512), ucarve(12800, 512)]; k_bfs = [ucarve(13312, 256, BF16), ucarve(13568, 256, BF16)]
        k_sq = [ucarve(13824, 512), ucarve(14336, 512)]; rtmps = [ucarve(14848, 256), ucarve(15104, 256)]
        m_bf = ucarve(35072, 1024, BF16)
        glu_f = ucarve(22528, 4352); glu = glu_f.rearrange("p (c j) -> p c j", c=2)
        sig = ucarve(26880, 2048); acc_f = ucarve(28928, 4096); acc = acc_f.rearrange("p (c t) -> p c t", c=2)
        ybf_f = ucarve(33024, 2048, BF16); ybf = ybf_f.rearrange("p (c t) -> p c t", c=2)
        gvs = [ucarve(0, 1024), ucarve(1024, 1024)]; vn_bfs = [ucarve(2048, 512, BF16), ucarve(2560, 512, BF16)]
        uT_f = ucarve(3072, 4096); uT = uT_f.rearrange("p (c t) -> p c t", c=2)
        gtmps = [ucarve(7168, 1024), ucarve(8192, 1024)]

        freeb = list(range(8))
        def bank():
            assert freeb, "no free PSUM bank"
            return banks[freeb.pop(0)][:]
        def rel(*aps):
            for ap in aps:
                i = int(ap.tensor.name[2:])
                assert i not in freeb
                freeb.append(i)

        def ACT(out, in_, func, r=None, w=None, **kw):
            return SC.op('act', lambda e: e.activation(out=out, in_=in_, func=func, **kw),
                         r=(r if r is not None else [in_]), w=(w if w is not None else [out]))
        def TT(eng, out, in0, in1, op, r=None, w=None):
            return SC.op(eng, lambda e: e.tensor_tensor(out=out, in0=in0, in1=in1, op=op),
                         r=(r if r is not None else [in0, in1]), w=(w if w is not None else [out]))
        def TS(eng, out, in0, s1, s2, op0, op1=None, r=None, w=None):
            rr = [in0] + [s for s in (s1, s2) if not isinstance(s, (int, float, type(None)))]
            if op1 is None:
                return SC.op(eng, lambda e: e.tensor_scalar(out=out, in0=in0, scalar1=s1, scalar2=None, op0=op0),
                             r=(r if r is not None else rr), w=(w if w is not None else [out]))
            return SC.op(eng, lambda e: e.tensor_scalar(out=out, in0=in0, scalar1=s1, scalar2=s2, op0=op0, op1=op1),
                         r=(r if r is not None else rr), w=(w if w is not None else [out]))
        def STT(out, in0, scalar, in1, op0, op1, r=None, w=None):
            rr = [in0, in1] + ([] if isinstance(scalar, (int, float)) else [scalar])
            return SC.op('dve', lambda e: e.scalar_tensor_tensor(out=out, in0=in0, scalar=scalar, in1=in1, op0=op0, op1=op1),
                         r=(r if r is not None else rr), w=(w if w is not None else [out]))
        def COPY(eng, out, in_):
            if eng == 'act':
                return ACT(out, in_, AF.Copy)
            return SC.op(eng, lambda e: e.tensor_copy(out=out, in_=in_), r=[in_], w=[out])
        def MEMSET(eng, ap, val):
            return SC.op(eng, lambda e: e.memset(ap, val), w=[ap])
        def MM(out, pairs, r=None, first=True, last=True, split=False):
            if split and len(pairs) > 1:
                for i, pr in enumerate(pairs):
                    MM(out, [pr], first=(first and i == 0), last=(last and i == len(pairs) - 1))
                return
            def fn(e):
                n = len(pairs)
                for i, (l, rh) in enumerate(pairs):
                    ins = e.matmul(out, lhsT=l, rhs=rh, start=(first and i == 0), stop=(last and i == n - 1))
                return ins
            rr = r if r is not None else [a for pr in pairs for a in pr]
            return SC.op('pe', fn, r=rr, w=[out])
        def TRANS(out_list, in_list, r, w):
            def fn(e):
                for o, i in zip(out_list, in_list):
                    ins = e.transpose(out=o, in_=i, identity=ident)
                return ins
            return SC.op('pe', fn, r=list(r) + [ident], w=w)
        def POW(ap):
            shp = list(ap.shape)
            return TT('pool', ap, ap, mhalf[:, 0:1].to_broadcast(shp) if len(shp) == 2 else mhalf[:, 0:1].unsqueeze(2).to_broadcast(shp), ALU.pow,
                      r=[ap, mhalf[:, 0:1]], w=[ap])

        epsc = mhalf
        def RSQ(out, in_, scale):
            ACT(out, in_, AF.Ln, scale=scale, bias=epsc[:, 0:1], r=[in_, epsc[:, 0:1]])
            ACT(out, out, AF.Exp, scale=-0.5)

        wstate = {'next': 0, 'store': {}}
        def w_src(l, i):
            name, kc0, nk, c0, ncols = slabs[i]
            Wv = wd[name][l].rearrange("(kc p) c -> p kc c", p=128)
            return Wv[:, kc0:kc0 + nk, c0:c0 + ncols], nk, ncols
        def w_issue(gidx):
            li, rem = divmod(gidx, NG * NS)
            if li >= len(layers): return
            l = layers[li]
            it, i = divmod(rem, NS)
            slot = gidx % NSLOT
            src, nk, ncols = w_src(l, i)
            dst = ring[slot][:, 0:nk * ncols].rearrange("p (k c) -> p k c", k=nk)
            if it == 0:
                SC.dma('pool', f'wq{slot}', lambda e, s: e.dma_start(out=dst, in_=src).then_inc(s, 16), w=[ring[slot]])
                sidx = l * NS + i
                stsem = f'wst{gidx % 4}'
                tk = SC.dma('sp', stsem, lambda e, s: e.dma_start(out=scr[sidx], in_=ring[slot]).then_inc(s, 16), r=[ring[slot]])
                wstate['store'][(l, i)] = tk
            else:
                sidx = l * NS + i
                SC.dma('sp', f'ws{slot}', lambda e, s: e.dma_start(out=ring[slot], in_=scr[sidx]).then_inc(s, 16),
                       w=[ring[slot]], extra=[wstate['store'].get((l, i))])
        wctr = [0]
        held = [None]
        def WHOLD(): held[0] = wctr[0]
        def WRELEASE(): held[0] = None
        def WGET(expect):
            g = wctr[0]; wctr[0] += 1
            base = g if held[0] is None else held[0]
            while wstate['next'] <= base + NSLOT - 1:
                w_issue(wstate['next']); wstate['next'] += 1
            i = g % NS
            name, kc0, nk, c0, ncols = slabs[i]
            assert (name, kc0, c0) == expect, (slabs[i], expect)
            slot = g % NSLOT
            return ring[slot][:, 0:nk * ncols].rearrange("p (k c) -> p k c", k=nk), ring[slot]

        xTv = xT.rearrange("(kc p) t -> p kc t", p=128)
        oTv = oT.rearrange("(kc p) t -> p kc t", p=128)
        def load_x(g):
            SC.dma('sp', f'xl{g % 2}', lambda e, s: e.dma_start(out=X[:, :, g * T:(g + 1) * T], in_=xTv[:, :, g * T:(g + 1) * T]).then_inc(s, 16),
                   w=[X[:, kc, g * T:(g + 1) * T] for kc in range(8)])
        load_x(0)
        if ngrun > 1: load_x(1)
        cst = ucarve(8192, 392 * 4)
        SC.dma('sp', 'p0', lambda e, s: e.dma_start(out=cst, in_=constd[:, :]).then_inc(s, 16), w=[cst])
        posi = ucarve(9792, 128, I32); posf = ucarve(9920, 128)
        SC.dma('sp', 'p1', lambda e, s: e.dma_start(out=posi, in_=posd[:, :]).then_inc(s, 16), w=[posi])
        COPY('dve', ident, cst[:, 0:128])
        for g4 in range(4):
            TS('dve', negp[:, g4 * 128:(g4 + 1) * 128], cst[:, 128:256], -1.0, 30000.0, ALU.add, ALU.mult)
            TS('dve', negn[:, g4 * 128:(g4 + 1) * 128], cst[:, 256:384], -1.0, 30000.0, ALU.add, ALU.mult)
        MEMSET('dve', ones, 1.0); MEMSET('dve', epsc, EPS)
        MEMSET('dve', va_f, 1.0)
        MEMSET('dve', stat, 0.0)
        COPY('dve', posf, posi)
        angt = ucarve(0, 1024); ang3 = angt.rearrange("p (n j) -> p n j", j=8)
        ut = ucarve(1024, 1024); ki = ucarve(2048, 1024, I32); kf = ucarve(3072, 1024)
        TT('dve', ang3, posf.unsqueeze(2).to_broadcast([128, 32, 8]), cst[:, 384:392].unsqueeze(1).to_broadcast([128, 32, 8]), ALU.mult,
           r=[posf, cst[:, 384:392]], w=[angt])
        for (dstT, shift) in ((sin_f, 0.0), (cos_f, 0.25)):
            TS('dve', ut, angt, 1.0 / (2 * math.pi), shift, ALU.mult, ALU.add)
            COPY('dve', ki, ut); COPY('dve', kf, ki)
            TT('dve', ut, ut, kf, ALU.subtract)
            TS('dve', ut, ut, 2 * math.pi, 3.14159, ALU.mult, ALU.min)
            TS('dve', ut, ut, -3.14159, None, ALU.max)
            ACT(dstT, ut, AF.Sin)

        for li, l in enumerate(layers):
            SC.dma('sp', 'p2', lambda e, s, l=l: e.dma_start(out=pcols, in_=pcolsd[l]).then_inc(s, 16), w=[pcols])
            SC.dma('sp', 'p3', lambda e, s, l=l: e.dma_start(out=prow, in_=prowd[l, 0, :].partition_broadcast(128)).then_inc(s, 16), w=[prow])
            for h in range(4):
                c, j = divmod(h, 2)
                SC.dma('sp', f'p{4 + h}', lambda e, s, l=l, h=h, c=c, j=j: e.dma_start(out=bsT[64 * j:64 * j + 64, c, :], in_=bsd[l, h, :].partition_broadcast(64)).then_inc(s, 16),
                       w=[bsT[:, c, :]])
            wstg = ucarve(0, 2048)
            SC.dma('sp', 'p8', lambda e, s, l=l: e.dma_start(out=wstg, in_=wsd[l]).then_inc(s, 16), w=[wstg])
            COPY('dve', wsT_f, wstg)
            ACT(esink[:, 0:8], prow[:, PR_SINK:PR_SINK + 8], AF.Exp)

            for it in range(ngrun):
                if li == 0 and it == 0: prestat = [False]
                t0 = it * T
                if stage == 0: SC.dead = True
                if li == 0 and it + 2 < ngrun:
                    load_x(it + 2)
                W = min(640, S - t0)

                def norm_pieces(Wn): return [(0, min(512, Wn))] + ([(512, Wn)] if Wn > 512 else [])
                def norm_stats(kc, Wn, tcol, bks, pieces):
                    ACT(sqb[:, kc % 2, 0:Wn], X[:, kc, tcol:tcol + Wn], AF.Square)
                    for (c0, c1), bk in zip(pieces, bks):
                        MM(bk[:, 0:c1 - c0], [(ones, sqb[:, kc % 2, c0:c1])], first=(kc == 0), last=(kc == 7))
                def norm_fin(bks, pieces, rbuf):
                    for (c0, c1), bk in zip(pieces, bks):
                        RSQ(rbuf[:, c0:c1], bk[:, 0:c1 - c0], 1.0 / D)
                    rel(*bks)
                def norm_apply(pc0, Wn, tcol, rbuf):
                    for kc in range(8):
                        STT(hT[:, kc, 0:Wn], X[:, kc, tcol:tcol + Wn], pcols[:, pc0 + kc:pc0 + kc + 1], rbuf[:, 0:Wn], ALU.mult, ALU.mult)

                if not prestat[0]:
                    pcs = norm_pieces(W); bks = [bank() for _ in pcs]
                    for kc in range(8): norm_stats(kc, W, t0, bks, pcs)
                    norm_fin(bks, pcs, rstd_n)
                norm_apply(PC_MIX, W, t0, rstd_n)
                prestat[0] = False
                if it + 1 < ngrun: nxt_t0 = (it + 1) * T
                elif li + 1 < len(layers): nxt_t0 = 0
                else: nxt_t0 = None

                if stage == 1: SC.dead = True
                kvt = [j for j in range(0 if it == 0 else 1, 5) if t0 + j * 128 < S]
                kvb = {}; kvbanks = []
                for idx, j in enumerate(kvt):
                    if idx % 2 == 0:
                        bcur = bank(); kvbanks.append(bcur)
                    kvb[j] = bcur[:, (idx % 2) * 256:(idx % 2) * 256 + 256]
                WHOLD()
                wkv = [WGET(('w_in', 4 * s2, 512))[0] for s2 in range(2)]
                for j in kvt:
                    MM(kvb[j], [(hT[:, 4 * s2 + kk, j * 128:(j + 1) * 128], wkv[s2][:, kk, :]) for s2 in range(2) for kk in range(4)], split=(j == kvt[0]))
                WRELEASE()

                def merge(*gens):
                    active = [[g, n] for g, n in gens]
                    while active:
                        for ent in list(active):
                            for _ in range(ent[1]):
                                try:
                                    next(ent[0])
                                except StopIteration:
                                    active.remove(ent); break

                def qk_prep(src_ps, nh, gofs, xfb, bfbuf, gtile, sqt, ss, rt):
                    n = nh * 64
                    xf = xfb[:, 0:n]
                    COPY('act', xf, src_ps); yield
                    TT('pool', sqt[:, 0:n], xf, xf, ALU.mult); yield
                    SC.op('dve', lambda e: e.tensor_reduce(out=ss, in_=sqt[:, 0:n].rearrange("p (h d) -> p h d", h=nh), axis=AX.X, op=ALU.add),
                          r=[sqt[:, 0:n]], w=[ss]); yield
                    ACT(ss, ss, AF.Ln, scale=1.0 / 64, bias=epsc[:, 0:1], r=[ss, epsc[:, 0:1]]); yield
                    ACT(ss, ss, AF.Exp, scale=-0.5); yield
                    x3 = xf.rearrange("p (h d) -> p h d", h=nh)
                    TT('dve', x3, x3, ss.unsqueeze(2).to_broadcast([128, nh, 64]), ALU.mult, r=[xf, ss], w=[xf]); yield
                    TT('dve', x3, x3, prow[:, gofs:gofs + 64].unsqueeze(1).to_broadcast([128, nh, 64]), ALU.mult,
                       r=[xf, prow[:, gofs:gofs + 64]], w=[xf]); yield
                    x1 = x3[:, :, 0:8]; x2 = x3[:, :, 8:16]
                    cb = cosT[:, gtile, :].unsqueeze(1).to_broadcast([128, nh, 8])
                    sb_ = sinT[:, gtile, :].unsqueeze(1).to_broadcast([128, nh, 8])
                    w8 = nh * 8
                    tt = [rt[:, i * w8:(i + 1) * w8].rearrange("p (h d) -> p h d", h=nh) for i in range(4)]
                    tr_ = [rt[:, 0:4 * w8]]
                    cr = [cosT[:, gtile, :]]; sr = [sinT[:, gtile, :]]
                    TT('pool', tt[0], x1, cb, ALU.mult, r=[xf] + cr, w=tr_); yield
                    TT('pool', tt[1], x2, sb_, ALU.mult, r=[xf] + sr, w=tr_); yield
                    TT('pool', tt[2], x2, cb, ALU.mult, r=[xf] + cr, w=tr_); yield
                    TT('pool', tt[3], x1, sb_, ALU.mult, r=[xf] + sr, w=tr_); yield
                    TT('pool', x1, tt[0], tt[1], ALU.subtract, r=tr_, w=[xf]); yield
                    TT('pool', x2, tt[2], tt[3], ALU.add, r=tr_, w=[xf]); yield
                    COPY('act', bfbuf[:, 0:n], xf); yield

                def kv_tile(j, par):
                    gt = 4 * it + j
                    slot = gt % 8
                    yield from qk_prep(kvb[j][:, 0:128], 2, PR_KG, k_fs[par], k_bfs[par], gt, k_sq[par], stat[:, 2 * par:2 * par + 2], rtmps[par])
                    tb = bank(); tbb = tb[:].bitcast(BF16)
                    TRANS([tbb[0:64, h * 128:(h + 1) * 128] for h in range(2)], [k_bfs[par][:, h * 64:(h + 1) * 64] for h in range(2)],
                          r=[k_bfs[par]], w=[tbb[0:64, 0:256]]); yield
                    SC.op('dve', lambda e, tbb=tbb, slot=slot: e.tensor_copy(out=kT[0:64, :, slot, :], in_=tbb[0:64, 0:256].rearrange("p (h t) -> p h t", h=2)),
                          r=[tbb[0:64, 0:256]], w=[kT[0:64, h, slot, :] for h in range(2)]); rel(tb); yield
                    COPY('act', vaug[:, slot, :, 0:64], kvb[j][:, 128:256].rearrange("p (h d) -> p h d", h=2)); yield
                for i2 in range(0, len(kvt), 2):
                    merge(*[(kv_tile(j, p), 1) for p, j in enumerate(kvt[i2:i2 + 2])])
                rel(*kvbanks)

                if stage == 2: SC.dead = True
                Nc = min(512, S - (t0 + 16))
                cvb = []; cvx = []; cvxb = []
                for c in range(4):
                    wv, wr = WGET(('w_in', 0, 1280 + 128 * c))
                    b = bank(); cvb.append(b)
                    MM(b[:, 0:Nc], [(wv[:, kc, :], hT[:, kc, 16:16 + Nc]) for kc in range(8)])
                    if it == 0:
                        if c % 2 == 0:
                            bx = bank(); cvxb.append(bx)
                        bxs = bx[:, (c % 2) * 16:(c % 2) * 16 + 16]
                        cvx.append(bxs)
                        MM(bxs, [(wv[:, kc, :], hT[:, kc, 0:16]) for kc in range(8)])
                if it == 0:
                    MEMSET('pool', glu[:, :, 0:16], 0.0)
                else:
                    COPY('pool', glu[:, :, 0:32], gtail)
                if Nc < 512:
                    MEMSET('pool', glu[:, :, 32 + Nc:544], 0.0)
                for c in range(2):
                    ACT(sig[:, 0:Nc], cvb[2 + c][:, 0:Nc], AF.Sigmoid)
                    TT('dve', glu[:, c, 32:32 + Nc], cvb[c][:, 0:Nc], sig[:, 0:Nc], ALU.mult)
                    if it == 0:
                        ACT(stat[:, 16:32], cvx[2 + c], AF.Sigmoid)
                        TT('dve', glu[:, c, 16:32], cvx[c], stat[:, 16:32], ALU.mult)
                COPY('pool', gtail, glu[:, :, 512:544])
                rel(*cvb); rel(*cvxb)
                def taps_gen():
                    for k in range(31):
                        for c in range(2):
                            cw = PC_CW + 31 * c
                            if k == 0:
                                TS('dve', acc[:, c, :], glu[:, c, 1:513], pcols[:, cw:cw + 1], pcols[:, PC_CB + c:PC_CB + c + 1], ALU.mult, ALU.add)
                            else:
                                STT(acc[:, c, :], glu[:, c, 1 + k:513 + k], pcols[:, cw + k:cw + k + 1], acc[:, c, :], ALU.mult, ALU.add)
                            yield
                taps = taps_gen()
                def BG(n):
                    for _ in range(n):
                        try: next(taps)
                        except StopIteration: return
                def taps_n(n):
                    for _ in range(n):
                        try: next(taps)
                        except StopIteration: return
                        yield

                def conv_finish():
                    for _ in taps: yield
                    b1 = bank(); b2 = bank()
                    for c in range(2):
                        COPY('act', ybf[:, c, :], acc[:, c, :]); yield
                        MM(b1, [(ones, ybf[:, c, :])], first=(c == 0), last=(c == 1)); yield
                    for c in range(2):
                        ACT(sqb[:, c, 0:512], acc[:, c, :], AF.Square); yield
                        MM(b2, [(ones, sqb[:, c, 0:512])], first=(c == 0), last=(c == 1)); yield
                    rc = rstd_b[:, 0:512]
                    TS('dve', sig, b1, 1.0 / 256, None, ALU.mult); yield
                    TS('dve', rc, b2, 1.0 / 256, None, ALU.mult); rel(b1, b2); yield
                    msq = ybf_f.bitcast(F32)[:, 0:512]
                    TT('pool', msq, sig, sig, ALU.mult, r=[sig], w=[ybf_f]); yield
                    TT('dve', rc, rc, msq, ALU.subtract, r=[rc, ybf_f], w=[rc]); yield
                    ACT(rc, rc, AF.Ln, scale=1.0, bias=epsc[:, 0:1], r=[rc, epsc[:, 0:1]]); yield
                    ACT(rc, rc, AF.Exp, scale=-0.5); yield
                    for c in range(2):
                        TT('dve', acc[:, c, :], acc[:, c, :], sig, ALU.subtract); yield
                        TT('dve', acc[:, c, :], acc[:, c, :], rc, ALU.mult); yield
                        ACT(acc[:, c, :], acc[:, c, :], AF.Silu, scale=pcols[:, PC_CLG + c:PC_CLG + c + 1], bias=pcols[:, PC_CLB + c:PC_CLB + c + 1],
                            r=[acc[:, c, :], pcols[:, PC_CLG:PC_CLB + 2]]); yield
                    b3 = bank()
                    for c in range(2):
                        ACT(sqb[:, c, 0:512], acc[:, c, :], AF.Square); yield
                        MM(b3, [(ones, sqb[:, c, 0:512])], first=(c == 0), last=(c == 1)); yield
                    ACT(rc, b3, AF.Ln, scale=1.0 / 256, bias=epsc[:, 0:1], r=[b3, epsc[:, 0:1]]); rel(b3); yield
                    ACT(rc, rc, AF.Exp, scale=-0.5); yield
                    for c in range(2):
                        STT(mergedT[:, 6 + c, :], acc[:, c, :], pcols[:, PC_OGC + c:PC_OGC + c + 1], rc, ALU.mult, ALU.mult); yield

                if stage == 3: SC.dead = True
                gvb = {}; gvbanks = []
                for j in range(4):
                    if j % 2 == 0:
                        bcur = bank(); gvbanks.append(bcur)
                    gvb[j] = bcur[:, (j % 2) * 256:(j % 2) * 256 + 256]
                WHOLD()
                wgv = [WGET(('w_in', 4 * s2, 1024))[0] for s2 in range(2)]
                for j in range(4):
                    MM(gvb[j], [(hT[:, 4 * s2 + kk, j * 128:(j + 1) * 128], wgv[s2][:, kk, :]) for s2 in range(2) for kk in range(4)])
                WRELEASE()
                gub = []
                for c in range(2):
                    wv, wr = WGET(('w_in', 0, 768 + 128 * c))
                    b = bank(); gub.append(b)
                    MM(b, [(wv[:, kc, :], hT[:, kc, 0:512]) for kc in range(8)])
                for c in range(2):
                    ACT(uT[:, c, :], gub[c], AF.Gelu)
                rel(*gub)
                def gm_tile(j, par):
                    g_ = gvs[par]; vb_ = vn_bfs[par]; gt_ = gtmps[par]
                    ACT(g_[:, 0:256], gvb[j], AF.Gelu); yield
                    st6 = stat[:, 32 + 16 * par:38 + 16 * par]; mv = stat[:, 40 + 16 * par:42 + 16 * par]
                    SC.op('dve', lambda e: e.bn_stats(out=st6, in_=g_[:, 0:256]), r=[g_], w=[st6]); yield
                    SC.op('dve', lambda e: e.bn_aggr(out=mv, in_=st6), r=[st6], w=[mv]); yield
                    rs = stat[:, 44 + 16 * par:45 + 16 * par]
                    ACT(rs, mv[:, 1:2], AF.Ln, scale=1.0, bias=epsc[:, 0:1], r=[mv, epsc[:, 0:1]]); yield
                    ACT(rs, rs, AF.Exp, scale=-0.5); yield
                    TS('dve', g_[:, 0:256], g_[:, 0:256], mv[:, 0:1], rs, ALU.subtract, ALU.mult); yield
                    TT('dve', g_[:, 0:256], g_[:, 0:256], prow[:, PR_LNG:PR_LNG + 256], ALU.mult); yield
                    TT('dve', vb_[:, 0:256], g_[:, 0:256], prow[:, PR_LNB:PR_LNB + 256], ALU.add); yield
                    sgb = bank()
                    for c in range(2):
                        MM(sgb[:, c * 256:(c + 1) * 256], [(vb_[:, c * 128:(c + 1) * 128], wsT[:, 2 * c:2 * c + 2, :])]); yield
                    for c in range(2):
                        for hh in range(2):
                            ps_ = slice(64 * hh, 64 * hh + 64)
                            src = sgb[ps_, c * 256 + hh * 128:c * 256 + hh * 128 + 128]
                            TT('dve', gt_[ps_, c * 128:c * 128 + 128], src, bsT[ps_, c, :], ALU.add); yield
                            TT('pool', uT[ps_, c, j * 128:(j + 1) * 128], uT[ps_, c, j * 128:(j + 1) * 128], gt_[ps_, c * 128:c * 128 + 128], ALU.mult); yield
                    rel(sgb)
                merge((gm_tile(0, 0), 1), (gm_tile(1, 1), 1), (taps_n(12), 1))
                merge((gm_tile(2, 0), 1), (gm_tile(3, 1), 1), (taps_n(12), 1))
                rel(*gvbanks)
                b4 = bank()
                for c in range(2):
                    ACT(sqb[:, c, 0:512], uT[:, c, :], AF.Square)
                    MM(b4, [(ones, sqb[:, c, 0:512])], first=(c == 0), last=(c == 1))
                RSQ(rstd_b[:, 0:512], b4, 1.0 / 256); rel(b4)
                for c in range(2):
                    STT(mergedT[:, 4 + c, :], uT[:, c, :], pcols[:, PC_OGG + c:PC_OGG + c + 1], rstd_b[:, 0:512], ALU.mult, ALU.mult)

                if stage == 4: SC.dead = True
                WHOLD()
                qw = [WGET(('w_in', 2 * s4, 0))[0] for s4 in range(4)]
                def att_prep(j):
                    gt = 4 * it + j; par = j % 2
                    qbj = bank()
                    MM(qbj, [(hT[:, 2 * s4 + kk, j * 128:(j + 1) * 128], qw[s4][:, kk, :]) for s4 in range(4) for kk in range(2)]); yield
                    first = True
                    for _ in qk_prep(qbj, 8, PR_QG, q_fs[par], q_bfs[par], gt, sqtmps[par], stat[:, 8 + 8 * par:16 + 8 * par], sqtmps[par]):
                        if first: rel(qbj); first = False
                        yield
                    tb = bank(); tbb = tb[:].bitcast(BF16)
                    TRANS([tbb[0:64, h * 128:(h + 1) * 128] for h in range(8)], [q_bfs[par][:, h * 64:(h + 1) * 64] for h in range(8)],
                          r=[q_bfs[par]], w=[tbb[0:64, :]]); yield
                    COPY('dve', qT_sbs[par][0:64, :], tbb[0:64, :]); rel(tb); yield
                obs = {}
                def att_A(j):
                    gt = 4 * it + j; par = j % 2
                    qT_sb = qT_sbs[par]
                    blocks = [b for b in (gt - 1, gt, gt + 1) if 0 <= b < 32]
                    for kvh in range(2):
                        for b in blocks:
                            bi = kvh * 3 + (b - gt + 1)
                            sb2 = bank()
                            pairs = [(kT[0:64, kvh, b % 8, :], qT_sb[0:64, kvh * 512:(kvh + 1) * 512])]
                            if b != gt:
                                pairs.append((ident, negp if b < gt else negn))
                            MM(sb2, pairs); yield
                            ACT(PT[:, bi, :], sb2, AF.Exp, scale=0.125); rel(sb2); yield
                    ob = [bank(), bank()]; obs[j] = ob
                    for h in range(8):
                        kvh, hq = divmod(h, 4)
                        outp = ob[h // 4][:, (h % 4) * 65:(h % 4) * 65 + 65]
                        MM(outp, [(PT[:, kvh * 3 + (b - gt + 1), hq * 128:(hq + 1) * 128], vaug[:, b % 8, kvh, :]) for b in blocks]); yield
                def att_B(j):
                    par = j % 2
                    ob = obs[j]
                    den = stat[:, 64 + 8 * par:72 + 8 * par]
                    for hb in range(2):
                        o3 = ob[hb][:, 0:260].rearrange("p (h d) -> p h d", h=4)
                        TT('dve', den[:, hb * 4:hb * 4 + 4], o3[:, :, 64], esink[:, hb * 4:hb * 4 + 4], ALU.add, r=[ob[hb][:, 0:260], esink], w=[den]); yield
                    SC.op('dve', lambda e, den=den: e.reciprocal(out=den, in_=den), r=[den], w=[den]); yield
                    for hb in range(2):
                        o3 = ob[hb][:, 0:260].rearrange("p (h d) -> p h d", h=4)
                        TT('dve', o_f[:, hb * 256:(hb + 1) * 256].rearrange("p (h d) -> p h d", h=4), o3[:, :, 0:64],
                           den[:, hb * 4:hb * 4 + 4].unsqueeze(2).to_broadcast([128, 4, 64]), ALU.mult,
                           r=[ob[hb][:, 0:260], den], w=[o_f[:, hb * 256:(hb + 1) * 256]]); yield
                    rel(*ob)
                    ss = stat[:, 80 + par:81 + par]
                    ACT(m_bf, o_f, AF.Square, accum_out=ss, w=[m_bf, ss]); yield
                    ACT(ss, ss, AF.Ln, scale=1.0 / 512, bias=epsc[:, 0:1], r=[ss, epsc[:, 0:1]]); yield
                    ACT(ss, ss, AF.Exp, scale=-0.5); yield
                    TS('dve', m_bf, o_f, ss, None, ALU.mult); yield
                    tb2 = bank(); tbb2 = tb2[:].bitcast(BF16)
                    TRANS([tbb2[:, c * 128:(c + 1) * 128] for c in range(4)], [m_bf[:, c * 128:(c + 1) * 128] for c in range(4)],
                          r=[m_bf], w=[tbb2[:, 0:512]]); yield
                    for c in range(4):
                        ACT(mergedT[:, c, j * 128:(j + 1) * 128], tbb2[:, c * 128:(c + 1) * 128], AF.Identity, scale=pcols[:, PC_OGA + c:PC_OGA + c + 1],
                            r=[tbb2[:, c * 128:(c + 1) * 128], pcols[:, PC_OGA:PC_OGA + 4]]); yield
                    rel(tb2)
                merge((att_prep(0), 1), (taps_n(8), 1))
                merge((att_prep(1), 1), (att_A(0), 1), (taps_n(15), 1))
                for j in range(4):
                    gens = [(att_B(j), 1)]
                    if j + 1 < 4: gens.append((att_A(j + 1), 2))
                    if j + 2 < 4: gens.append((att_prep(j + 2), 1))
                    if j == 0: gens.append((taps_n(15), 1))
                    if j == 1: gens.append((conv_finish(), 2))
                    merge(*gens)

                if stage == 5: SC.dead = True
                WRELEASE()
                for m in range(8):
                    wv, wr = WGET(('w_out', 0, 128 * m))
                    b = bank()
                    MM(b, [(wv[:, kc, :], mergedT[:, kc, :]) for kc in range(8)])
                    TT('dve', X[:, m, t0:t0 + T], b, X[:, m, t0:t0 + T], ALU.add); rel(b)
                    if m == 0:
                        pcsF = norm_pieces(T); bksF = [bank() for _ in pcsF]
                    else:
                        norm_stats(m - 1, T, t0, bksF, pcsF)
                norm_stats(7, T, t0, bksF, pcsF)
                norm_fin(bksF, pcsF, rstd_b)
                ACT(stat[:, 100:101], stat[:, 100:101], AF.Silu)
                norm_apply(PC_FFN, T, t0, rstd_b)

                if stage == 6: SC.dead = True
                for (f0, f1) in ((0, 16), (16, 22)):
                    for f in range(f0, f1):
                        wg, _ = WGET(('w_gate_up', 0, 128 * f))
                        bg = bank()
                        MM(bg, [(wg[:, kc, :], hT[:, kc, 0:T]) for kc in range(8)], split=(f == 0))
                        wu, _ = WGET(('w_gate_up', 0, DFF + 128 * f))
                        bu = bank()
                        MM(bu, [(wu[:, kc, :], hT[:, kc, 0:T]) for kc in range(8)])
                        sgt = sg2[f % 2]
                        ACT(sgt, bg, AF.Silu); rel(bg)
                        TT('dve', actb[:, f - f0, :], bu, sgt, ALU.mult); rel(bu)
                        if nxt_t0 is not None and f < 8:
                            Wn_ = min(640, S - nxt_t0)
                            if f == 0:
                                pcsN = norm_pieces(Wn_); bksN = [bank() for _ in pcsN]
                            norm_stats(f, Wn_, nxt_t0, bksN, pcsN)
                            if f == 7:
                                norm_fin(bksN, pcsN, rstd_n); prestat[0] = True
                    for m in range(8):
                        b = bank()
                        pieces = list(range(f0, f1, 8))
                        for pi, k0 in enumerate(pieces):
                            nk = min(8, f1 - k0)
                            wv, _ = WGET(('w_down', k0, 128 * m))
                            MM(b, [(wv[:, kk, :], actb[:, k0 - f0 + kk, :]) for kk in range(nk)],
                               first=(pi == 0), last=(pi == len(pieces) - 1), split=(m == 0))
                        TT('dve', X[:, m, t0:t0 + T], b, X[:, m, t0:t0 + T], ALU.add); rel(b)
                        if f0 == 16:
                            if m == 0:
                                pcsP = norm_pieces(T); bksP = [bank() for _ in pcsP]
                            else:
                                norm_stats(m - 1, T, t0, bksP, pcsP)

                if stage == 7: SC.dead = True
                norm_stats(7, T, t0, bksP, pcsP)
                norm_fin(bksP, pcsP, rstd_b)
                ACT(stat[:, 101:102], stat[:, 101:102], AF.Sigmoid)
                norm_apply(PC_PLE, T, t0, rstd_b)
                pv = pTd[l].rearrange("(c p) t -> p c t", p=128)
                SC.dma('pool', 'pl', lambda e, s, pv=pv, t0=t0: e.dma_start(out=pTb, in_=pv[:, :, t0:t0 + T]).then_inc(s, 16), w=[pTb_f])
                for m in range(8):
                    if m % 4 == 0:
                        WHOLD()
                        pjw, _ = WGET(('w_ple_proj', 0, 512 * (m // 4)))
                    wg, _ = WGET(('w_ple_gate', 0, 128 * m))
                    bg = bank()
                    MM(bg, [(wg[:, kc, :], hT[:, kc, 0:T]) for kc in range(8)], split=(m == 0))
                    bp = bank()
                    MM(bp, [(pjw[:, kk, (m % 4) * 128:(m % 4) * 128 + 128], pTb[:, kk, :]) for kk in range(2)])
                    if m % 4 == 3: WRELEASE()
                    sgt = sg2[m % 2]
                    ACT(sgt, bg, AF.Sigmoid); rel(bg)
                    TT('dve', ptmp[m % 2], bp, sgt, ALU.mult); rel(bp)
                    TT('pool', X[:, m, t0:t0 + T], X[:, m, t0:t0 + T], ptmp[m % 2], ALU.add)
                SC.dead = False; SC.maxrec = 10**9
                if li == len(layers) - 1:
                    SC.dma('pool', 'out', lambda e, s, t0=t0: e.dma_start(out=oTv[:, :, t0:t0 + T], in_=X[:, :, t0:t0 + T]).then_inc(s, 16),
                           r=[X[:, kc, t0:t0 + T] for kc in range(8)])
        SC.wait('sp', ('out', SC.cnt['out']))
        SC.wait('pool', ('out', SC.cnt['out']))
        for s in [f'wst{i}' for i in range(4)]:
            if SC.cnt[s]: SC.wait('sp', (s, SC.cnt[s]))
        SC.replay(block)
    return nc


def _host_layout(inp):
    f32 = np.float32
    g = {k: np.asarray(v) for k, v in inp.items()}
    L = DEPTH
    pcols = np.zeros((L, 128, NPC), f32)
    prow = np.zeros((L, 1, NPR), f32)
    for l in range(L):
        def col(v, n): return np.ascontiguousarray(v.reshape(n, 128).T)
        pcols[l, :, PC_MIX:PC_MIX + 8] = col(g['norm_mix_g'][l], 8)
        pcols[l, :, PC_FFN:PC_FFN + 8] = col(g['norm_ffn_g'][l], 8)
        pcols[l, :, PC_PLE:PC_PLE + 8] = col(g['ple_norm_g'][l], 8)
        og = g['out_norm_g'][l]
        pcols[l, :, PC_OGA:PC_OGA + 4] = col(og[0:512], 4)
        pcols[l, :, PC_OGG:PC_OGG + 2] = col(og[512:768], 2)
        pcols[l, :, PC_OGC:PC_OGC + 2] = col(og[768:1024], 2)
        cw = g['conv_w'][l]
        for c in range(2):
            pcols[l, :, PC_CW + 31 * c:PC_CW + 31 * c + 31] = cw[:, c * 128:(c + 1) * 128].T
        pcols[l, :, PC_CB:PC_CB + 2] = col(g['conv_b'][l], 2)
        pcols[l, :, PC_CLG:PC_CLG + 2] = col(g['conv_ln_g'][l], 2)
        pcols[l, :, PC_CLB:PC_CLB + 2] = col(g['conv_ln_b'][l], 2)
        prow[l, 0, PR_QG:PR_QG + 64] = g['q_norm_g'][l]
        prow[l, 0, PR_KG:PR_KG + 64] = g['k_norm_g'][l]
        prow[l, 0, PR_SINK:PR_SINK + 8] = g['sink'][l]
        prow[l, 0, PR_LNG:PR_LNG + 256] = g['gm_ln_g'][l]
        prow[l, 0, PR_LNB:PR_LNB + 256] = g['gm_ln_b'][l]
    wsT = np.ascontiguousarray(g['gm_ws'].transpose(0, 3, 1, 2)).reshape(L, 128, 512).astype(f32)
    consts = np.zeros((128, 392), f32)
    consts[:, 0:128] = np.eye(128, dtype=f32)
    jj = np.arange(128)[:, None]; ii = np.arange(128)[None, :]
    consts[:, 128:256] = (jj >= ii).astype(f32)
    consts[:, 256:384] = (jj <= ii).astype(f32)
    consts[:, 384:392] = (500000.0 ** (-np.arange(0, 16, 2, dtype=f32) / 16)).astype(f32)[None, :]
    common = dict(consts=consts, pcols=pcols, prow=prow, gm_bs=np.ascontiguousarray(g['gm_bs'], f32), wsT=wsT)
    for k in ('w_in', 'w_out', 'w_gate_up', 'w_down', 'w_ple_gate', 'w_ple_proj'):
        common[k] = np.ascontiguousarray(g[k], f32)
    maps = []
    for c in range(NCORES):
        m = dict(common)
        m['xT'] = np.ascontiguousarray(g['x'][c].T)
        m['pT'] = np.ascontiguousarray(g['p'][:, c].transpose(0, 2, 1))
        m['pos'] = np.ascontiguousarray(g['positions'][c].reshape(32, 128).T).astype(np.int32)
        maps.append(m)
    return maps


_NC_CACHE = {}


def kernel(**inputs):
    maps = _host_layout(inputs)
    key = 'fused'
    if key not in _NC_CACHE:
        _NC_CACHE[key] = build_nc([0, 1], True)
    nc = _NC_CACHE[key]
    res = run_bass_kernel_spmd(nc, maps, core_ids=list(range(NCORES)))
    out = np.stack([np.ascontiguousarray(res.results[c]["oT"].T) for c in range(NCORES)], axis=0)
    return out.astype(np.float32)
```
